# Optimizing a Trainium2 kernel written in Bass

```python
import jax
import jax.numpy as jnp
from jax import lax
import numpy as np

D_MODEL = 1024
BATCH = 8
SEQ = 8192
DEPTH = 4

FOX_HEADS = 8
FOX_HEAD_DIM = 64
FOX_WIDTH = FOX_HEADS * FOX_HEAD_DIM
Q_BLOCK = 128
GLA_HEADS = 4
GLA_KEY_DIM = 64
GLA_VAL_DIM = 128
GLA_K_WIDTH = GLA_HEADS * GLA_KEY_DIM
GLA_V_WIDTH = GLA_HEADS * GLA_VAL_DIM
GLA_GATE_RANK = 16
GLA_TAU = 16.0
GLA_CHUNK = 64
POOL_WINDOWS = (2, 4, 8, 16)
POOL_GROUPS = len(POOL_WINDOWS)
POOL_GROUP_DIM = 128
POOL_WIDTH = POOL_GROUPS * POOL_GROUP_DIM
N_BRANCHES = 3
RMS_EPS = 1e-6
IN_SPLITS = (FOX_WIDTH, FOX_WIDTH, FOX_WIDTH, FOX_HEADS,
             GLA_K_WIDTH, GLA_K_WIDTH, GLA_V_WIDTH, GLA_GATE_RANK,
             POOL_WIDTH,
             FOX_WIDTH, GLA_V_WIDTH, POOL_WIDTH,
             N_BRANCHES * D_MODEL)
IN_WIDTH = sum(IN_SPLITS)

kernel_name = 'hybrid_fox_gla_pool_gated_block'


def rms_norm(x, g):
    xf = x.astype(jnp.float32)
    y = xf * lax.rsqrt(jnp.mean(xf * xf, axis=-1, keepdims=True) + RMS_EPS)
    return (y * g.astype(jnp.float32)).astype(x.dtype)


def fox_attention(q, k, v, log_f):
    b, s, h, dh = q.shape
    nb = s // Q_BLOCK
    f_cum = jnp.cumsum(log_f.astype(jnp.float32), axis=1).transpose(0, 2, 1)
    kt = k.astype(jnp.float32).transpose(0, 2, 1, 3)
    vt = v.astype(jnp.float32).transpose(0, 2, 1, 3)
    qb = (q.astype(jnp.float32) * dh ** -0.5).reshape(b, nb, Q_BLOCK, h, dh).transpose(1, 0, 3, 2, 4)
    fb = f_cum.reshape(b, h, nb, Q_BLOCK).transpose(2, 0, 1, 3)
    key_pos = jnp.arange(s)

    def one_block(args):
        qi, fi, blk = args
        logits = jnp.einsum('bhqd,bhkd->bhqk', qi, kt)
        logits = logits + fi[..., :, None] - f_cum[:, :, None, :]
        q_pos = blk * Q_BLOCK + jnp.arange(Q_BLOCK)
        causal = key_pos[None, :] <= q_pos[:, None]
        probs = jax.nn.softmax(jnp.where(causal, logits, -jnp.inf), axis=-1)
        return jnp.einsum('bhqk,bhkd->bhqd', probs, vt)

    out = lax.map(one_block, (qb, fb, jnp.arange(nb)))
    return out.transpose(1, 0, 3, 2, 4).reshape(b, s, h * dh).astype(q.dtype)


def gla_chunked(q, k, v, log_a):
    b, s, h, dk = q.shape
    dv = v.shape[-1]
    nc = s // GLA_CHUNK

    def to_chunks(t):
        return t.astype(jnp.float32).reshape(b, nc, GLA_CHUNK, h, t.shape[-1]).transpose(1, 0, 3, 2, 4)

    qc = to_chunks(q) * dk ** -0.5
    kc, vc, ac = to_chunks(k), to_chunks(v), to_chunks(log_a)
    causal = jnp.tril(jnp.ones((GLA_CHUNK, GLA_CHUNK), dtype=bool))[:, :, None]

    def step(state, inp):
        qi, ki, vi, ai = inp
        cum = jnp.cumsum(ai, axis=2)
        last = cum[:, :, -1:, :]
        o_inter = jnp.einsum('bhtd,bhde->bhte', qi * jnp.exp(cum), state)
        rel = cum[:, :, :, None, :] - cum[:, :, None, :, :]
        decay = jnp.exp(jnp.where(causal, rel, -jnp.inf))
        scores = jnp.einsum('bhtd,bhsd,bhtsd->bhts', qi, ki, decay)
        o_intra = jnp.einsum('bhts,bhse->bhte', scores, vi)
        new_state = (jnp.exp(last[:, :, 0, :])[..., None] * state
                     + jnp.einsum('bhsd,bhse->bhde', ki * jnp.exp(last - cum), vi))
        return new_state, o_inter + o_intra

    state0 = jnp.zeros((b, h, dk, dv), jnp.float32)
    _, out = lax.scan(step, state0, (qc, kc, vc, ac))
    return out.transpose(1, 0, 3, 2, 4).reshape(b, s, h, dv)


def pool_mixer(u, w_pool, pool_scale):
    s = u.shape[1]
    uf = u.astype(jnp.float32)
    csum = jnp.cumsum(uf, axis=1)
    count = jnp.arange(1, s + 1, dtype=jnp.float32)
    outs = []
    for g, w in enumerate(POOL_WINDOWS):
        lo, hi = g * POOL_GROUP_DIM, (g + 1) * POOL_GROUP_DIM
        cs = csum[..., lo:hi]
        lagged = jnp.pad(cs[:, :s - w], ((0, 0), (w, 0), (0, 0)))
        mean = (cs - lagged) / jnp.minimum(count, float(w))[None, :, None]
        diff = (mean - uf[..., lo:hi]).astype(u.dtype)
        outs.append(jnp.einsum('bsc,cd->bsd', diff, w_pool[g]))
    return jnp.concatenate(outs, axis=-1) * pool_scale


def hybrid_layer(x, c, w_ada, b_ada, g_pre, g_post, w_in, b_forget, w_gla_gate, b_gla_gate,
                 g_gla_norm, w_pool, pool_scale, w_br_fox, w_br_gla, w_br_pool, w_out):
    b, s, _ = x.shape
    mod = jnp.einsum('bd,de->be', jax.nn.silu(c), w_ada) + b_ada
    shift, scale, gate = jnp.split(mod[:, None, :], 3, axis=-1)
    h = rms_norm(x, g_pre) * (1.0 + scale) + shift

    proj = jnp.einsum('bsd,de->bse', h, w_in)
    split_idx = np.cumsum(IN_SPLITS)[:-1].tolist()
    (fq, fk, fv, ff, gq, gk, gv, g_lr, pu,
     z_fox, z_gla, z_pool, merge) = jnp.split(proj, split_idx, axis=-1)

    fox_shape = (b, s, FOX_HEADS, FOX_HEAD_DIM)
    log_f = jax.nn.log_sigmoid((ff + b_forget).astype(jnp.float32))
    o_fox = fox_attention(fq.reshape(fox_shape), fk.reshape(fox_shape), fv.reshape(fox_shape), log_f)

    gate_logits = jnp.einsum('bsr,rk->bsk', g_lr, w_gla_gate) + b_gla_gate
    log_a = jax.nn.log_sigmoid(gate_logits.astype(jnp.float32)) / GLA_TAU
    k_shape = (b, s, GLA_HEADS, GLA_KEY_DIM)
    o_gla = gla_chunked(gq.reshape(k_shape), gk.reshape(k_shape),
                        gv.reshape(b, s, GLA_HEADS, GLA_VAL_DIM), log_a.reshape(k_shape))
    o_gla = rms_norm(o_gla, g_gla_norm.reshape(GLA_HEADS, GLA_VAL_DIM)).reshape(b, s, GLA_V_WIDTH).astype(x.dtype)

    o_pool = pool_mixer(pu, w_pool, pool_scale).astype(x.dtype)

    y_fox = jnp.einsum('bse,ed->bsd', o_fox * jax.nn.silu(z_fox), w_br_fox)
    y_gla = jnp.einsum('bse,ed->bsd', o_gla * jax.nn.silu(z_gla), w_br_gla)
    y_pool = jnp.einsum('bse,ed->bsd', o_pool * jax.nn.silu(z_pool), w_br_pool)
    m_fox, m_gla, m_pool = jnp.split(merge, 3, axis=-1)
    merged = jax.nn.sigmoid(m_fox) * y_fox + jax.nn.sigmoid(m_gla) * y_gla + jax.nn.sigmoid(m_pool) * y_pool
    out = jnp.einsum('bsd,de->bse', merged, w_out)

    return x + gate * rms_norm(out, g_post)


def setup_inputs(seed: int = 0) -> dict:
    key = jax.random.key(seed)
    ks = jax.random.split(key, 18)
    f32 = jnp.float32

    def nrm(k, shape, std):
        return jax.random.normal(k, shape, f32) * std

    return {
        'x': nrm(ks[0], (BATCH, SEQ, D_MODEL), 1.0),
        'c': nrm(ks[1], (BATCH, D_MODEL), 1.0),
        'w_ada': nrm(ks[2], (DEPTH, D_MODEL, 3 * D_MODEL), 0.5 * D_MODEL ** -0.5),
        'b_ada': nrm(ks[3], (DEPTH, 3 * D_MODEL), 0.01),
        'g_pre': 1.0 + nrm(ks[4], (DEPTH, D_MODEL), 0.1),
        'g_post': 1.0 + nrm(ks[5], (DEPTH, D_MODEL), 0.1),
        'w_in': nrm(ks[6], (DEPTH, D_MODEL, IN_WIDTH), D_MODEL ** -0.5),
        'b_forget': 2.0 + nrm(ks[7], (DEPTH, FOX_HEADS), 0.5),
        'w_gla_gate': nrm(ks[8], (DEPTH, GLA_GATE_RANK, GLA_K_WIDTH), GLA_GATE_RANK ** -0.5),
        'b_gla_gate': nrm(ks[9], (DEPTH, GLA_K_WIDTH), 0.1),
        'g_gla_norm': 1.0 + nrm(ks[10], (DEPTH, GLA_V_WIDTH), 0.1),
        'w_pool': nrm(ks[11], (DEPTH, POOL_GROUPS, POOL_GROUP_DIM, POOL_GROUP_DIM), POOL_GROUP_DIM ** -0.5),
        'pool_scale': 1.0 + nrm(ks[12], (DEPTH, POOL_WIDTH), 0.1),
        'w_br_fox': nrm(ks[13], (DEPTH, FOX_WIDTH, D_MODEL), FOX_WIDTH ** -0.5),
        'w_br_gla': nrm(ks[14], (DEPTH, GLA_V_WIDTH, D_MODEL), GLA_V_WIDTH ** -0.5),
        'w_br_pool': nrm(ks[15], (DEPTH, POOL_WIDTH, D_MODEL), POOL_WIDTH ** -0.5),
        'w_out': nrm(ks[16], (DEPTH, D_MODEL, D_MODEL), D_MODEL ** -0.5),
    }


def reference(x, c, w_ada, b_ada, g_pre, g_post, w_in, b_forget, w_gla_gate, b_gla_gate,
              g_gla_norm, w_pool, pool_scale, w_br_fox, w_br_gla, w_br_pool, w_out):
    h = x
    for i in range(DEPTH):
        h = hybrid_layer(h, c, w_ada[i], b_ada[i], g_pre[i], g_post[i], w_in[i], b_forget[i],
                         w_gla_gate[i], b_gla_gate[i], g_gla_norm[i], w_pool[i], pool_scale[i],
                         w_br_fox[i], w_br_gla[i], w_br_pool[i], w_out[i])
    return h
```

```python
import numpy as np
import ml_dtypes
from contextlib import ExitStack
import concourse.bass as bass
import concourse.mybir as mybir
from concourse.bass_utils import run_bass_kernel_spmd

F32 = mybir.dt.float32
BF16 = mybir.dt.bfloat16
AF = mybir.ActivationFunctionType
ALU = mybir.AluOpType

ENGS = ['pe', 'act', 'dve', 'pool', 'sp']
SEG = 30000
NSEG = 8
NDSEM = 12
D = 1024
IN_W = 7704
EPS = 1e-6
NEG = -30000.0

SEGS = [
    ('fq', 0, 512, 'f', 'q'), ('fk', 512, 512, 'f', 'copy'), ('fv', 1024, 512, 't', 'copy'),
    ('ff', 1536, 8, 'ff', None),
    ('gq', 1544, 256, 'f', 'copy'), ('gk', 1800, 256, 'f', 'copy'), ('gv', 2056, 512, 't', 'copy'),
    ('glr', 2568, 16, 'f', 'copy'), ('pu', 2584, 512, 'f', 'copy'),
    ('zf', 3096, 512, 'f', 'silu'), ('zg', 3608, 512, 'f', 'silu'), ('zp', 4120, 512, 'f', 'silu'),
    ('mg', 4632, 3072, 'f', 'sigmoid'),
]


class Buf:
    __slots__ = ('w', 'r')

    def __init__(self):
        self.w = None
        self.r = []


class Op:
    __slots__ = ('eng', 'fn', 'deps', 'inc', 'cnt', 'seg', 'is_dma', 'dsem', 'dcnt', 'gen')


class Prog:
    def __init__(self, nc, es):
        self.nc = nc
        self.q = {e: [] for e in ENGS}
        self.csem = {e: [es.enter_context(nc.semaphore(f"c_{e}_{i}")) for i in range(NSEG)] for e in ENGS if e != 'sp'}
        self.dsem = {e: [es.enter_context(nc.semaphore(f"d_{e}_{i}")) for i in range(NDSEM)] for e in ('sp', 'pool')}
        self.ccount = {e: 0 for e in ENGS}
        self.dcount = {e: 0 for e in ENGS}
        self.waited = {e: {} for e in ENGS}
        self.bufs = []
        self.nwaits = 0
        self.ninst = 0
        self.gen = 0

    def buf(self):
        b = Buf()
        self.bufs.append(b)
        return b

    def bufs_n(self, n):
        return [self.buf() for _ in range(n)]

    def op(self, eng, fn, reads=(), writes=(), deps=()):
        o = Op()
        o.eng = eng
        o.fn = fn
        o.inc = False
        o.is_dma = False
        o.cnt = None
        d = [x for x in deps if x is not None]
        for b in reads:
            if b.w is not None:
                d.append(b.w)
        for b in writes:
            if b.w is not None:
                d.append(b.w)
            last = {}
            for r in b.r:
                if r.is_dma:
                    d.append(r)
                else:
                    last[r.eng] = r
            d.extend(last.values())
        for b in reads:
            b.r.append(o)
        for b in writes:
            b.w = o
            b.r = []
        o.gen = self.gen
        o.deps = [x for x in d if x.gen == self.gen and not (x.eng == 'pe' and eng == 'pe' and not x.is_dma)]
        self.q[eng].append(o)
        return o

    def dma(self, eng, out, in_, reads=(), writes=(), deps=()):
        o = self.op(eng, lambda e: e.dma_start(out=out, in_=in_), reads, writes, deps)
        o.is_dma = True
        return o

    def flush(self):
        nc = self.nc
        alld = [o for e in ENGS for o in self.q[e] if o.is_dma]
        if alld:
            self.op('sp', lambda e: e.nop(), deps=alld)
        for e in ENGS:
            for o in self.q[e]:
                for d in o.deps:
                    if not d.is_dma:
                        d.inc = True
        for e in ENGS:
            for o in self.q[e]:
                if o.is_dma:
                    k = self.dcount[e]
                    o.dsem = self.dsem[e][k % NDSEM]
                    o.dcnt = 16 * (k // NDSEM + 1)
                    self.dcount[e] = k + 1
                elif o.inc:
                    n = self.ccount[e]
                    o.seg = n // SEG
                    o.cnt = n % SEG + 1
                    self.ccount[e] = n + 1
                    assert o.seg < NSEG
        csem = self.csem

        def run(e, eng):
            waited = self.waited[e]
            for o in self.q[e]:
                need = {}
                for d in o.deps:
                    if d.is_dma:
                        key, val = d.dsem, d.dcnt
                    else:
                        key, val = csem[d.eng][d.seg], d.cnt
                    if need.get(key, 0) < val:
                        need[key] = val
                if o.is_dma and o.dcnt > 16:
                    if need.get(o.dsem, 0) < o.dcnt - 16:
                        need[o.dsem] = o.dcnt - 16
                for key, val in need.items():
                    if waited.get(key, 0) >= val:
                        continue
                    eng.wait_ge(key, val)
                    waited[key] = val
                    self.nwaits += 1
                ins = o.fn(eng)
                self.ninst += 1
                if o.is_dma:
                    ins.then_inc(o.dsem, 16)
                elif o.inc:
                    ins.then_inc(csem[e][o.seg], 1)

        with nc.Block() as block:
            @block.tensor
            def _(eng):
                run('pe', eng)

            @block.scalar
            def _(eng):
                run('act', eng)

            @block.vector
            def _(eng):
                run('dve', eng)

            @block.gpsimd
            def _(eng):
                run('pool', eng)

            @block.sync
            def _(eng):
                run('sp', eng)
        nc.all_engine_barrier()
        self.gen += 1
        self.q = {e: [] for e in ENGS}
        for b in self.bufs:
            b.w = None
            b.r = []
        self.bufs = []


def build(S, DEPTH, dbg=(), phases=('M', 'A', 'B1', 'B2', 'C')):
    NT = S // 512
    NB = S // 128
    SUP = min(S, 4096)
    NSUP = S // SUP
    nc = bass.Bass("TRN2", target_bir_lowering=False)

    def din(name, shape, dt=F32):
        return nc.dram_tensor(name, shape, dt, kind="ExternalInput").ap()

    def dscr(name, shape, dt=BF16):
        kind = "ExternalOutput" if name in dbg else "Internal"
        return nc.dram_tensor(name, shape, dt, kind=kind).ap()

    x = din("x", [S, D])
    ct = din("ct", [128, 8])
    w_ada = din("w_ada", [DEPTH, D, 3 * D])
    b_ada_col = din("b_ada_col", [DEPTH, 128, 24])
    b_ada_row = din("b_ada_row", [DEPTH, 1, 3 * D])
    g_pre_col = din("g_pre_col", [DEPTH, 128, 8])
    g_post_row = din("g_post_row", [DEPTH, 1, D])
    w_in = din("w_in", [DEPTH, D, IN_W])
    b_forget_col = din("b_forget_col", [DEPTH, 8, 1])
    w_gg = din("w_gla_gate", [DEPTH, 16, 256])
    b_gla_col = din("b_gla_col", [DEPTH, 64, 4])
    g_gla_col = din("g_gla_col", [DEPTH, 128, 4])
    w_pool = din("w_pool", [DEPTH, 4, 128, 128])
    pscale_col = din("pscale_col", [DEPTH, 128, 4])
    w_brs = [din(n, [DEPTH, 512, D]) for n in ("w_br_fox", "w_br_gla", "w_br_pool")]
    w_out = din("w_out", [DEPTH, D, D])
    c_ident = din("c_ident", [128, 128], BF16)
    c_maskA = din("c_maskA", [128, 4, 512], BF16)
    c_cm4 = din("c_cm4", [128, 4, 128], BF16)
    c_segm = din("c_segm", [64, 512])
    c_rcfix = din("c_rcfix", [128, 4, 16])
    y = nc.dram_tensor("y", [S, D], F32, kind="ExternalOutput").ap()

    qT_s = dscr("qT_s", [512, S]); kT_s = dscr("kT_s", [512, S]); v_s = dscr("v_s", [S, 512])
    augq_s = dscr("augq_s", [8, 3, S]); augk_s = dscr("augk_s", [8, 3, S])
    gqT_s = dscr("gqT_s", [256, S]); gkT_s = dscr("gkT_s", [256, S]); gv_s = dscr("gv_s", [S, 512])
    glrT_s = dscr("glrT_s", [16, S]); puT_s = dscr("puT_s", [512, S])
    szfT_s = dscr("szfT_s", [512, S]); szgT_s = dscr("szgT_s", [512, S]); szpT_s = dscr("szpT_s", [512, S])
    sgmT_s = dscr("sgmT_s", [3072, S])
    gfT_s = dscr("gfT_s", [512, S]); ggT_s = dscr("ggT_s", [512, S])
    xbuf = [dscr("xbuf0", [S, D], F32), dscr("xbuf1", [S, D], F32)]
    seg_dst = {'fq': qT_s, 'fk': kT_s, 'fv': v_s, 'gq': gqT_s, 'gk': gkT_s, 'gv': gv_s, 'glr': glrT_s,
               'pu': puT_s, 'zf': szfT_s, 'zg': szgT_s, 'zp': szpT_s, 'mg': sgmT_s}

    with ExitStack() as ges:
        P = Prog(nc, ges)

        def gsb(name, shape, dt):
            return ges.enter_context(nc.sbuf_tensor(name, shape, dt))

        pbig = [ges.enter_context(nc.psum_tensor(f"pbig{i}", [128, 1024], F32)) for i in range(4)]
        pb = [pbig[i // 2][:, (i % 2) * 512:(i % 2 + 1) * 512] for i in range(8)]
        ptv = pbig[3].bitcast(BF16)
        ptbs = [ptv[:, 0:1024], ptv[:, 1024:2048]]
        ptb = ptbs[0]
        ident = gsb("ident", [128, 128], BF16)
        maskA = gsb("maskA", [128, 4, 512], BF16)
        cm4 = gsb("cm4", [128, 4, 128], BF16)
        segm = gsb("segm", [64, 512], F32)
        rcfix = gsb("rcfix", [128, 4, 16], F32)
        ones_f = gsb("ones_f", [128, 512], F32)
        ones_bf = gsb("ones_bf", [128, 128], BF16)
        epsc = gsb("epsc", [128, 1], F32)
        onec = gsb("onec", [128, 1], F32)
        sc = gsb("sc", [128, 8], F32)
        scb = gsb("scb", [128, 8, 128], F32)
        ctt = gsb("ctt", [128, 8], F32)
        Acol = gsb("Acol", [128, 8], F32)
        Bcol = gsb("Bcol", [128, 8], F32)
        G2 = gsb("G2", [128, D], F32)
        bacol = gsb("bacol", [128, 24], F32)
        gprec = gsb("gprec", [128, 8], F32)
        tmp8 = gsb("tmp8", [128, 8], F32)

        P.dma('sp', ident[:], c_ident)
        P.dma('sp', maskA[:], c_maskA)
        P.dma('sp', cm4[:], c_cm4)
        P.dma('sp', segm[:], c_segm)
        P.dma('sp', rcfix[:], c_rcfix)
        d_ct = P.dma('sp', ctt[:], ct)
        P.op('pool', lambda e: e.memset(ones_f[:], 1.0))
        P.op('pool', lambda e: e.memset(ones_bf[:], 1.0))
        P.op('pool', lambda e: e.memset(epsc[:], EPS))
        P.op('pool', lambda e: e.memset(onec[:], 1.0))
        o_sc = P.op('act', lambda e: e.activation(out=sc[:], in_=ctt[:], func=AF.Silu), deps=[d_ct])
        P.flush()
        for k in range(8):
            P.op('dve', lambda e, k=k: e.tensor_scalar(out=scb[:, k, :], in0=ones_f[:, 0:128], scalar1=sc[:, k:k + 1],
                                                       scalar2=None, op0=ALU.mult))
        P.flush()

        for l in range(DEPTH):
            xsrc = x if l == 0 else xbuf[(l - 1) % 2]
            xdst = y if l == DEPTH - 1 else xbuf[l % 2]

            with ExitStack() as es:
              if 'M' in phases:
                def sb(name, shape, dt):
                    return es.enter_context(nc.sbuf_tensor(f"{name}_{l}", shape, dt))
                wa = [sb(f"wa{i}", [128, 8, 512], F32) for i in range(2)]
                wa_b = P.bufs_n(2)
                gpb = sb("gpb", [128, D], F32)
                bgb = sb("bgb", [128, D], F32)
                modps = pb[0]
                b_mod = P.buf()
                gps = [pb[1], pb[2]]
                b_gps = P.bufs_n(2)
                wav = w_ada[l].rearrange("(k p) n -> p k n", p=128)
                d1 = P.dma('pool', bacol[:], b_ada_col[l])
                d2 = P.dma('pool', gprec[:], g_pre_col[l])
                d3 = P.dma('pool', gpb[:], g_post_row[l].partition_broadcast(128))
                d4 = P.dma('pool', bgb[:], b_ada_row[l][:, 2 * D:3 * D].partition_broadcast(128))
                for nb in range(6):
                    s = nb % 2
                    P.dma('sp', wa[s][:], wav[:, :, nb * 512:(nb + 1) * 512], writes=[wa_b[s]])
                    if nb < 4:
                        for j in range(4):
                            ci = nb * 4 + j
                            for k in range(8):
                                P.op('pe', lambda e, s=s, j=j, k=k, ci=ci: e.matmul(
                                    modps[:, ci:ci + 1], lhsT=wa[s][:, k, j * 128:(j + 1) * 128], rhs=sc[:, k:k + 1],
                                    start=(k == 0), stop=(k == 7)), reads=[wa_b[s]], writes=[b_mod])
                    else:
                        hf = nb - 4
                        for k in range(8):
                            P.op('pe', lambda e, s=s, k=k, hf=hf: e.matmul(
                                gps[hf][:, :], lhsT=scb[:, k, :], rhs=wa[s][:, k, :], start=(k == 0), stop=(k == 7)),
                                reads=[wa_b[s]], writes=[b_gps[hf]])
                P.op('dve', lambda e: e.tensor_tensor(out=Bcol[:], in0=modps[:, 0:8], in1=bacol[:, 0:8], op=ALU.add),
                     reads=[b_mod], deps=[d1])
                o_t8 = P.op('dve', lambda e: e.tensor_tensor(out=tmp8[:], in0=modps[:, 8:16], in1=bacol[:, 8:16], op=ALU.add),
                            reads=[b_mod], deps=[d1])
                P.op('dve', lambda e: e.scalar_tensor_tensor(out=Acol[:], in0=tmp8[:], scalar=1.0, in1=gprec[:],
                                                             op0=ALU.add, op1=ALU.mult), deps=[o_t8, d2])
                for hf in range(2):
                    o1 = P.op('dve', lambda e, hf=hf: e.tensor_tensor(out=G2[:, hf * 512:(hf + 1) * 512], in0=gps[hf][:, :],
                                                                      in1=bgb[:, hf * 512:(hf + 1) * 512], op=ALU.add),
                              reads=[b_gps[hf]], deps=[d4])
                    P.op('dve', lambda e, hf=hf: e.tensor_tensor(out=G2[:, hf * 512:(hf + 1) * 512],
                                                                 in0=G2[:, hf * 512:(hf + 1) * 512],
                                                                 in1=gpb[:, hf * 512:(hf + 1) * 512], op=ALU.mult),
                         deps=[o1, d3])
                P.flush()

            with ExitStack() as es:
              if 'A' in phases:
                def sb(name, shape, dt):
                    return es.enter_context(nc.sbuf_tensor(f"{name}_{l}", shape, dt))
                hT = sb("hT", [128, 8, SUP], BF16)
                wst = [sb(f"wst{i}", [128, 8, 512], F32) for i in range(2)]
                wbf = [sb(f"wbf{i}", [128, 8, 512], BF16) for i in range(2)]
                xin = [sb(f"xin{i}", [128, D], F32) for i in range(3)]
                xn = [sb(f"xn{i}", [128, D], BF16) for i in range(2)]
                sqj = sb("sqj", [128, D], BF16)
                ssq = [sb(f"ssq{i}", [128, 1], F32) for i in range(2)]
                lnv = [sb(f"lnv{i}", [128, 1], F32) for i in range(2)]
                rstd = [sb(f"rstd{i}", [128, 1], F32) for i in range(2)]
                stage = [sb(f"stage{i}", [128, 512], BF16) for i in range(4)]
                nbf = sb("nbf", [8, 1], F32)
                bfc = sb("bfc", [8, 1], F32)
                ft1 = sb("ft1", [8, 512], F32)
                ft2 = sb("ft2", [8, 512], F32)
                Fneg = [sb(f"Fneg{i}", [8, 512], F32) for i in range(2)]
                r1 = sb("r1", [8, 512], F32)
                r2 = sb("r2", [8, 512], F32)
                hk = [sb(f"hk{i}", [8, 3, 512], BF16) for i in range(2)]
                hq = [sb(f"hq{i}", [8, 3, 512], BF16) for i in range(2)]
                d_bf = P.dma('pool', bfc[:], b_forget_col[l])
                P.op('pool', lambda e: e.tensor_scalar(out=nbf[:], in0=bfc[:], scalar1=-1.0, scalar2=None, op0=ALU.mult),
                     deps=[d_bf])
                P.flush()
                winv = w_in[l].rearrange("(k p) n -> p k n", p=128)
                fcount = 0
                for sup in range(NSUP):
                    t0 = sup * SUP
                    xin_b = P.bufs_n(3); xn_b = P.bufs_n(2); ss_b = P.bufs_n(2); ln_b = P.bufs_n(2); rs_b = P.bufs_n(2)
                    tp_bs = P.bufs_n(2); sqj_b = P.buf()
                    for tb in range(0 if 'noA1' in phases else SUP // 128):
                        s = tb % 2
                        x3 = tb % 3
                        if tb == 0:
                            P.dma('sp', xin[0][:], xsrc[t0:t0 + 128, :], writes=[xin_b[0]])
                        if tb + 1 < SUP // 128:
                            r1_ = t0 + (tb + 1) * 128
                            P.dma('sp', xin[(tb + 1) % 3][:], xsrc[r1_:r1_ + 128, :], writes=[xin_b[(tb + 1) % 3]])
                        P.op('pool', lambda e, s=s: e.memset(ssq[s][:], 0.0), writes=[ss_b[s]])
                        P.op('act', lambda e, s=s, x3=x3: e.activation(out=sqj[:], in_=xin[x3][:], func=AF.Square, accum_out=ssq[s][:, 0:1]),
                             reads=[xin_b[x3]], writes=[sqj_b, ss_b[s]])
                        P.op('act', lambda e, s=s: e.activation(out=lnv[s][:], in_=ssq[s][:], func=AF.Ln, bias=epsc[:, 0:1], scale=1.0 / D),
                             reads=[ss_b[s]], writes=[ln_b[s]])
                        P.op('act', lambda e, s=s: e.activation(out=rstd[s][:], in_=lnv[s][:], func=AF.Exp, scale=-0.5),
                             reads=[ln_b[s]], writes=[rs_b[s]])
                        P.op('dve', lambda e, s=s, x3=x3: e.tensor_scalar(out=xn[s][:], in0=xin[x3][:], scalar1=rstd[s][:, 0:1], scalar2=None,
                                                                   op0=ALU.mult), reads=[xin_b[x3], rs_b[s]], writes=[xn_b[s]])
                        tp_b = tp_bs[s]
                        for k in range(8):
                            P.op('pe', lambda e, s=s, k=k: e.transpose(ptbs[s][:, k * 128:(k + 1) * 128], xn[s][:, k * 128:(k + 1) * 128], ident[:]),
                                 reads=[xn_b[s]], writes=[tp_b])
                        for k in range(8):
                            dst = hT[:, k, tb * 128:(tb + 1) * 128]
                            src = ptbs[s][:, k * 128:(k + 1) * 128]
                            if s == 0:
                                P.op('act', lambda e, dst=dst, src=src, k=k: e.activation(out=dst, in_=src, func=AF.Identity,
                                                                                        bias=Bcol[:, k:k + 1], scale=Acol[:, k:k + 1]),
                                     reads=[tp_b])
                            else:
                                P.op('dve', lambda e, dst=dst, src=src, k=k: e.tensor_scalar(out=dst, in0=src, scalar1=Acol[:, k:k + 1],
                                                                                           scalar2=Bcol[:, k:k + 1], op0=ALU.mult, op1=ALU.add),
                                     reads=[tp_b])
                    P.flush()
                    wst_b = P.bufs_n(2); wbf_b = P.bufs_n(2); st_b = P.bufs_n(4); acc_b = P.bufs_n(6)
                    ft1_b = P.buf(); ft2_b = P.buf(); F_b = P.bufs_n(2); r1_b = P.buf(); r2_b = P.buf()
                    hk_b = P.bufs_n(2); hq_b = P.bufs_n(2)
                    blk = 0
                    acc_i = 0
                    st_i = 0
                    segsel = [sg for sg in SEGS if not any(p.startswith('seg:') for p in phases) or ('seg:' + sg[0]) in phases]
                    for (name, c0, width, layout, evac) in segsel:
                        for off in range(0, width, 512):
                            w = min(512, width - off)
                            s = blk % 2
                            blk += 1
                            P.dma('pool', wst[s][:, :, 0:w], winv[:, :, c0 + off:c0 + off + w], writes=[wst_b[s]])
                            P.op('pool', lambda e, s=s, w=w: e.tensor_copy(out=wbf[s][:, :, 0:w], in_=wst[s][:, :, 0:w]),
                                 reads=[wst_b[s]], writes=[wbf_b[s]])
                            if layout == 'f':
                                dst = seg_dst[name]
                                for j in range((w + 127) // 128):
                                    M = min(128, w - j * 128)
                                    row0 = off + j * 128
                                    for t in range(SUP // 512):
                                        a = acc_i % 6; acc_i += 1
                                        for k in range(8):
                                            P.op('pe', lambda e, a=a, s=s, k=k, j=j, M=M, t=t: e.matmul(
                                                pb[a][0:M, :], lhsT=wbf[s][:, k, j * 128:j * 128 + M], rhs=hT[:, k, t * 512:(t + 1) * 512],
                                                start=(k == 0), stop=(k == 7)), reads=[wbf_b[s]], writes=[acc_b[a]])
                                        g = st_i % 4; st_i += 1
                                        if evac == 'silu' or evac == 'sigmoid':
                                            fn = AF.Silu if evac == 'silu' else AF.Sigmoid
                                            P.op('act', lambda e, a=a, g=g, M=M, fn=fn: e.activation(out=stage[g][0:M, :], in_=pb[a][0:M, :], func=fn),
                                                 reads=[acc_b[a]], writes=[st_b[g]])
                                        elif evac == 'q':
                                            P.op('dve', lambda e, a=a, g=g, M=M: e.tensor_scalar(out=stage[g][0:M, :], in0=pb[a][0:M, :], scalar1=0.125,
                                                                                                 scalar2=None, op0=ALU.mult),
                                                 reads=[acc_b[a]], writes=[st_b[g]])
                                        else:
                                            P.op('dve', lambda e, a=a, g=g, M=M: e.tensor_copy(out=stage[g][0:M, :], in_=pb[a][0:M, :]),
                                                 reads=[acc_b[a]], writes=[st_b[g]])
                                        P.dma('sp', dst[row0:row0 + M, t0 + t * 512:t0 + (t + 1) * 512], stage[g][0:M, :], reads=[st_b[g]])
                            elif layout == 't':
                                dst = seg_dst[name]
                                for tb in range(SUP // 128):
                                    a = acc_i % 6; acc_i += 1
                                    for k in range(8):
                                        P.op('pe', lambda e, a=a, s=s, k=k, tb=tb: e.matmul(
                                            pb[a][:, :], lhsT=hT[:, k, tb * 128:(tb + 1) * 128], rhs=wbf[s][:, k, 0:512],
                                            start=(k == 0), stop=(k == 7)), reads=[wbf_b[s]], writes=[acc_b[a]])
                                    g = st_i % 4; st_i += 1
                                    P.op('dve', lambda e, a=a, g=g: e.tensor_copy(out=stage[g][:, :], in_=pb[a][:, :]),
                                         reads=[acc_b[a]], writes=[st_b[g]])
                                    P.dma('sp', dst[t0 + tb * 128:t0 + (tb + 1) * 128, :], stage[g][:, :], reads=[st_b[g]])
                            else:
                                for t in range(SUP // 512):
                                    a = acc_i % 6; acc_i += 1
                                    for k in range(8):
                                        P.op('pe', lambda e, a=a, s=s, k=k, t=t: e.matmul(
                                            pb[a][0:8, :], lhsT=wbf[s][:, k, 0:8], rhs=hT[:, k, t * 512:(t + 1) * 512],
                                            start=(k == 0), stop=(k == 7)), reads=[wbf_b[s]], writes=[acc_b[a]])
                                    f = fcount % 2
                                    P.op('act', lambda e, a=a: e.activation(out=ft1[:], in_=pb[a][0:8, :], func=AF.Exp, bias=nbf[:, 0:1], scale=-1.0),
                                         reads=[acc_b[a]], writes=[ft1_b])
                                    P.op('act', lambda e: e.activation(out=ft2[:], in_=ft1[:], func=AF.Ln, bias=onec[0:8, 0:1], scale=1.0),
                                         reads=[ft1_b], writes=[ft2_b])
                                    if fcount == 0:
                                        P.op('dve', lambda e, f=f: e.tensor_tensor_scan(out=Fneg[f][:], data0=ones_f[0:8, :], data1=ft2[:], initial=0.0,
                                                                                        op0=ALU.mult, op1=ALU.add), reads=[ft2_b], writes=[F_b[f]])
                                    else:
                                        P.op('dve', lambda e, f=f: e.tensor_tensor_scan(out=Fneg[f][:], data0=ones_f[0:8, :], data1=ft2[:],
                                                                                        initial=Fneg[1 - f][:, 511:512], op0=ALU.mult, op1=ALU.add),
                                             reads=[ft2_b, F_b[1 - f]], writes=[F_b[f]])
                                    P.op('dve', lambda e, f=f: e.tensor_copy(out=hk[f][:, 0, :], in_=Fneg[f][:]), reads=[F_b[f]], writes=[hk_b[f]])
                                    P.op('dve', lambda e, f=f: e.tensor_tensor(out=r1[:], in0=Fneg[f][:], in1=hk[f][:, 0, :], op=ALU.subtract),
                                         reads=[F_b[f], hk_b[f]], writes=[r1_b])
                                    P.op('dve', lambda e, f=f: e.tensor_copy(out=hk[f][:, 1, :], in_=r1[:]), reads=[r1_b], writes=[hk_b[f]])
                                    P.op('dve', lambda e, f=f: e.tensor_tensor(out=r2[:], in0=r1[:], in1=hk[f][:, 1, :], op=ALU.subtract),
                                         reads=[r1_b, hk_b[f]], writes=[r2_b])
                                    P.op('dve', lambda e, f=f: e.tensor_copy(out=hk[f][:, 2, :], in_=r2[:]), reads=[r2_b], writes=[hk_b[f]])
                                    P.op('dve', lambda e, f=f: e.tensor_scalar(out=hq[f][:], in0=hk[f][:], scalar1=-1.0, scalar2=None, op0=ALU.mult),
                                         reads=[hk_b[f]], writes=[hq_b[f]])
                                    cs = slice(t0 + t * 512, t0 + (t + 1) * 512)
                                    P.dma('sp', augk_s[:, :, cs], hk[f][:], reads=[hk_b[f]])
                                    P.dma('sp', augq_s[:, :, cs], hq[f][:], reads=[hq_b[f]])
                                    fcount += 1
                    P.flush()

            with ExitStack() as es:
              if 'B1' in phases:
                def sb(name, shape, dt):
                    return es.enter_context(nc.sbuf_tensor(f"{name}_{l}", shape, dt))
                NQ = 3
                KT = [sb(f"KT{i}", [70, S], BF16) for i in range(2)]
                VA = [sb(f"VA{i}", [128, NB, 128], BF16) for i in range(2)]
                QT = [sb(f"QT{i}", [70, 512], BF16) for i in range(NQ)]
                szf = [sb(f"szf{i}", [64, 512], BF16) for i in range(NQ)]
                pt = [sb(f"pt{i}", [128, 1024], BF16) for i in range(3)]
                rec = [sb(f"rec{i}", [128, 512], F32) for i in range(2)]
                gn = [sb(f"gn{i}", [64, 512], F32) for i in range(2)]
                gf = [sb(f"gf{i}", [64, 512], BF16) for i in range(NQ)]
                KT_b = P.bufs_n(2); VA_b = P.bufs_n(2); QT_b = P.bufs_n(NQ); szf_b = P.bufs_n(NQ); pt_b = P.bufs_n(3)
                st_b = P.bufs_n(3); ot_b = P.bufs_n(2); rec_b = P.bufs_n(2); gn_b = P.bufs_n(2); gf_b = P.bufs_n(NQ)
                for i in range(2):
                    P.op('pool', lambda e, i=i: e.memset(KT[i][64:70, :], 1.0), writes=[KT_b[i]])
                    P.op('pool', lambda e, i=i: e.memset(VA[i][:, :, 64:128], 1.0), writes=[VA_b[i]])
                for i in range(NQ):
                    P.op('pool', lambda e, i=i: e.memset(QT[i][64:70, :], 1.0), writes=[QT_b[i]])
                vview = v_s.rearrange("(kb p) c -> p kb c", p=128)
                tiles = [(h, T) for h in range(8) for T in range(NT)]

                def load_head(h):
                    ks = h % 2
                    P.dma('sp', KT[ks][0:64, :], kT_s[h * 64:(h + 1) * 64, :], writes=[KT_b[ks]])
                    P.dma('sp', KT[ks][67:70, :], augk_s[h], writes=[KT_b[ks]])
                    for c in range(0, NB, 16):
                        ce = min(NB, c + 16)
                        P.dma('sp', VA[ks][:, c:ce, 0:64], vview[:, c:ce, h * 64:(h + 1) * 64], writes=[VA_b[ks]])

                def load_tile(i):
                    h, T = tiles[i]
                    qs = i % NQ
                    cs = slice(T * 512, (T + 1) * 512)
                    P.dma('sp', QT[qs][0:64, :], qT_s[h * 64:(h + 1) * 64, cs], writes=[QT_b[qs]])
                    P.dma('sp', QT[qs][64:67, :], augq_s[h][:, cs], writes=[QT_b[qs]])
                    P.dma('sp', szf[qs][:, :], szfT_s[h * 64:(h + 1) * 64, cs], writes=[szf_b[qs]])

                sti = 0
                load_head(0)
                load_tile(0)
                for i, (h, T) in enumerate(tiles):
                    ks = h % 2
                    qs = i % NQ
                    if T == 0 and h + 1 < 8:
                        load_head(h + 1)
                    if i + 1 < len(tiles):
                        load_tile(i + 1)
                    cs = slice(T * 512, (T + 1) * 512)
                    nkb = 4 * T + 4
                    npair = nkb // 2
                    o = 6 + (i % 2); ob = ot_b[i % 2]
                    w2 = i % 2
                    sts = []

                    def emit_st(pp_):
                        nonlocal sti
                        a = sti % 3; sti += 1
                        for hf in range(2):
                            kb = 2 * pp_ + hf
                            diag = kb >= 4 * T
                            P.op('pe', lambda e, a=a, kb=kb, diag=diag, ks=ks, qs=qs, hf=hf: e.matmul(
                                pbig[a][:, hf * 512:(hf + 1) * 512], lhsT=KT[ks][0:70, kb * 128:(kb + 1) * 128], rhs=QT[qs][0:70, :], start=True, stop=not diag),
                                reads=[KT_b[ks], QT_b[qs]], writes=[st_b[a]])
                            if diag:
                                P.op('pe', lambda e, a=a, j=kb - 4 * T, hf=hf: e.matmul(
                                    pbig[a][:, hf * 512:(hf + 1) * 512], lhsT=ident[:, :], rhs=maskA[:, j, :], start=False, stop=True), writes=[st_b[a]])
                        sts.append(a)

                    emit_st(0)
                    if npair > 1:
                        emit_st(1)
                    for pp_ in range(npair):
                        a = sts[pp_]
                        P.op('act', lambda e, a=a: e.activation(out=pt[a][:, :], in_=pbig[a][:, :], func=AF.Exp),
                             reads=[st_b[a]], writes=[pt_b[a]])
                        if pp_ + 2 < npair:
                            emit_st(pp_ + 2)
                        for hf in range(2):
                            kb = 2 * pp_ + hf
                            P.op('pe', lambda e, a=a, kb=kb, o=o, ks=ks, nkb=nkb, hf=hf: e.matmul(
                                pb[o][:, :], lhsT=VA[ks][:, kb, :], rhs=pt[a][:, hf * 512:(hf + 1) * 512], start=(kb == 0), stop=(kb == nkb - 1)),
                                reads=[VA_b[ks], pt_b[a]], writes=[ob])
                    P.op('dve', lambda e, o=o, w2=w2: e.reciprocal(out=rec[w2][64:128, :], in_=pb[o][64:128, :]), reads=[ob], writes=[rec_b[w2]])
                    P.op('dve', lambda e, o=o, w2=w2: e.tensor_tensor(out=gn[w2][:, :], in0=pb[o][0:64, :], in1=rec[w2][64:128, :], op=ALU.mult),
                         reads=[ob, rec_b[w2]], writes=[gn_b[w2]])
                    P.op('pool', lambda e, qs=qs, w2=w2: e.tensor_tensor(out=gf[qs][:, :], in0=gn[w2][:, :], in1=szf[qs][:, :], op=ALU.mult),
                         reads=[gn_b[w2], szf_b[qs]], writes=[gf_b[qs]])
                    P.dma('pool', gfT_s[h * 64:(h + 1) * 64, cs], gf[qs][:, :], reads=[gf_b[qs]])
                P.flush()

            with ExitStack() as es:
              if 'B2' in phases:
                def sb(name, shape, dt):
                    return es.enter_context(nc.sbuf_tensor(f"{name}_{l}", shape, dt))
                gq = [sb(f"gq{i}", [64, 4, 512], BF16) for i in range(2)]
                gk = [sb(f"gk{i}", [64, 4, 512], BF16) for i in range(2)]
                glr = [sb(f"glr{i}", [16, 512], BF16) for i in range(2)]
                gv = [sb(f"gv{i}", [128, 4, 512], BF16) for i in range(2)]
                szg = [sb(f"szg{i}", [128, 4, 512], BF16) for i in range(2)]
                wg32 = sb("wg32", [16, 256], F32)
                wgb = sb("wgb", [16, 256], BF16)
                bgl = sb("bgl", [64, 4], F32)
                nbg = sb("nbg", [64, 4], F32)
                ggc = sb("ggc", [128, 4], F32)
                e1 = sb("e1", [64, 4, 512], F32)
                la = sb("la", [64, 4, 512], F32)
                cum = sb("cum", [64, 4, 512], F32)
                eq = sb("eq", [64, 4, 512], F32)
                ek = sb("ek", [64, 4, 512], F32)
                qt = sb("qt", [64, 4, 512], BF16)
                kt = sb("kt", [64, 4, 512], BF16)
                ktok = sb("ktok", [128, 16, 64], BF16)
                scm = [sb(f"scm{i}", [128, 4, 128], BF16) for i in range(4)]
                sq = [sb(f"sq{i}", [128, 512], BF16) for i in range(2)]
                l1 = sb("l1", [128, 512], F32)
                rr = sb("rr", [128, 512], F32)
                t1 = sb("t1", [128, 512], F32)
                gg = [sb(f"gg{i}", [128, 512], BF16) for i in range(2)]
                Sf = sb("Sf", [64, 4, 128], F32)
                Sbf = sb("Sbf", [64, 4, 128], BF16)
                tmpS = sb("tmpS", [64, 128], F32)
                in_b = P.bufs_n(2)
                e1_b = P.buf(); la_b = P.buf(); cum_b = P.buf(); eq_b = P.buf(); ek_b = P.buf(); qt_b = P.buf(); kt_b = P.buf()
                ktok_b = P.buf(); tpk_b = P.buf(); scm_b = P.bufs_n(4); sq_b = P.bufs_n(2); l1_b = P.buf(); rr_b = P.buf(); t1_b = P.buf()
                gg_b = P.bufs_n(2); Sf_b = P.bufs_n(4); Sbf_b = P.bufs_n(4); tmpS_b = P.buf()
                otb = P.bufs_n(4)
                scb_ = P.buf()
                U_b = P.buf()
                aux_b = U_b
                d_w = P.dma('pool', wg32[:], w_gg[l])
                d_b = P.dma('pool', bgl[:], b_gla_col[l])
                d_g = P.dma('pool', ggc[:], g_gla_col[l])
                P.op('pool', lambda e: e.tensor_copy(out=wgb[:], in_=wg32[:]), deps=[d_w])
                P.op('pool', lambda e: e.tensor_scalar(out=nbg[:], in0=bgl[:], scalar1=-1.0, scalar2=None, op0=ALU.mult), deps=[d_b])
                P.op('pool', lambda e: e.memset(Sf[:], 0.0))
                P.op('pool', lambda e: e.memset(Sbf[:], 0.0))
                P.flush()
                gqv = gqT_s.rearrange("(h d) s -> d h s", d=64)
                gkv = gkT_s.rearrange("(h d) s -> d h s", d=64)
                gvv = gv_s.rearrange("(c p) e -> p c e", p=128)
                szgv = szgT_s.rearrange("(h e) s -> e h s", e=128)
                ggi = 0
                def b2_load(T):
                    s = T % 2
                    cs = slice(T * 512, (T + 1) * 512)
                    P.dma('sp', gq[s][:], gqv[:, :, cs], writes=[in_b[s]])
                    P.dma('sp', gk[s][:], gkv[:, :, cs], writes=[in_b[s]])
                    P.dma('sp', glr[s][:], glrT_s[:, cs], writes=[in_b[s]])
                    P.dma('sp', gv[s][:], gvv[:, T * 4:(T + 1) * 4, :], writes=[in_b[s]])
                    P.dma('sp', szg[s][:], szgv[:, :, cs], writes=[in_b[s]])

                b2_load(0)
                for T in range(NT):
                    s = T % 2
                    cs = slice(T * 512, (T + 1) * 512)
                    if T + 1 < NT:
                        b2_load(T + 1)
                    for h in range(4):
                        P.op('pe', lambda e, h=h, s=s: e.matmul(pb[5][0:64, :], lhsT=wgb[0:16, h * 64:(h + 1) * 64], rhs=glr[s][0:16, :],
                                                                start=True, stop=True), reads=[in_b[s]], writes=[aux_b])
                        P.op('act', lambda e, h=h: e.activation(out=e1[:, h, :], in_=pb[5][0:64, :], func=AF.Exp, bias=nbg[:, h:h + 1], scale=-1.0),
                             reads=[aux_b], writes=[e1_b])
                    P.op('act', lambda e: e.activation(out=la[:], in_=e1[:], func=AF.Ln, bias=onec[0:64, 0:1], scale=1.0),
                         reads=[e1_b], writes=[la_b])
                    for h in range(4):
                        P.op('dve', lambda e, h=h: e.tensor_tensor_scan(out=cum[:, h, :], data0=segm[:, :], data1=la[:, h, :], initial=0.0,
                                                                        op0=ALU.mult, op1=ALU.add), reads=[la_b], writes=[cum_b])
                    P.op('act', lambda e: e.activation(out=eq[:], in_=cum[:], func=AF.Exp, scale=-1.0 / 16.0), reads=[cum_b], writes=[eq_b])
                    P.op('act', lambda e: e.activation(out=ek[:], in_=cum[:], func=AF.Exp, scale=1.0 / 16.0), reads=[cum_b], writes=[ek_b])
                    P.op('dve', lambda e, s=s: e.scalar_tensor_tensor(out=qt[:], in0=gq[s][:], scalar=0.125, in1=eq[:], op0=ALU.mult, op1=ALU.mult),
                         reads=[in_b[s], eq_b], writes=[qt_b])
                    P.op('dve', lambda e, s=s: e.tensor_tensor(out=kt[:], in0=gk[s][:], in1=ek[:], op=ALU.mult),
                         reads=[in_b[s], ek_b], writes=[kt_b])
                    for c in range(4):
                        for h in range(4):
                            i = c * 4 + h
                            P.op('pe', lambda e, c=c, h=h, i=i: e.transpose(ptb[:, i * 64:(i + 1) * 64], kt[0:64, h, c * 128:(c + 1) * 128], ident[0:64, 0:64]),
                                 reads=[kt_b], writes=[tpk_b])
                    P.op('act', lambda e: e.activation(out=ktok[:].rearrange("p a b -> p (a b)"), in_=ptb[:, :], func=AF.Copy),
                         reads=[tpk_b], writes=[ktok_b])
                    for h in range(4):
                        for c in range(4):
                            P.op('pe', lambda e, h=h, c=c: e.matmul(pb[4][:, c * 128:(c + 1) * 128], lhsT=kt[0:64, h, c * 128:(c + 1) * 128],
                                                                    rhs=qt[0:64, h, c * 128:(c + 1) * 128], start=True, stop=True),
                                 reads=[kt_b, qt_b], writes=[scb_])
                        P.op('dve', lambda e, h=h: e.tensor_tensor(out=scm[h][:].rearrange("p a b -> p (a b)"), in0=pb[4][:, :],
                                                                   in1=cm4[:].rearrange("p a b -> p (a b)"), op=ALU.mult),
                             reads=[scb_], writes=[scm_b[h]])
                    for c in range(4):
                        for h in range(4):
                            ccs = slice(c * 128, (c + 1) * 128)
                            P.op('pe', lambda e, h=h, ccs=ccs: e.matmul(pb[h][:, ccs], lhsT=Sbf[0:64, h, :], rhs=qt[0:64, h, ccs], start=True, stop=False),
                                 reads=[Sbf_b[h], qt_b], writes=[otb[h]])
                            P.op('pe', lambda e, h=h, c=c, ccs=ccs, s=s: e.matmul(pb[h][:, ccs], lhsT=gv[s][:, c, h * 128:(h + 1) * 128], rhs=scm[h][:, c, :],
                                                                              start=False, stop=True),
                                 reads=[scm_b[h], in_b[s]], writes=[otb[h]])
                            P.op('pe', lambda e, h=h, c=c, s=s: e.matmul(pb[5][0:64, 0:128], lhsT=ktok[:, c * 4 + h, :], rhs=gv[s][:, c, h * 128:(h + 1) * 128],
                                                                         start=True, stop=True),
                                 reads=[ktok_b, in_b[s]], writes=[U_b])
                            P.op('dve', lambda e, h=h: e.tensor_tensor(out=tmpS[:, :], in0=pb[5][0:64, 0:128], in1=Sf[:, h, :], op=ALU.add),
                                 reads=[U_b, Sf_b[h]], writes=[tmpS_b])
                            P.op('dve', lambda e, h=h, c=c: e.tensor_scalar(out=Sf[:, h, :], in0=tmpS[:, :], scalar1=eq[:, h, c * 128 + 127:c * 128 + 128],
                                                                            scalar2=None, op0=ALU.mult),
                                 reads=[tmpS_b, eq_b], writes=[Sf_b[h]])
                            P.op('pool', lambda e, h=h: e.tensor_copy(out=Sbf[:, h, :], in_=Sf[:, h, :]), reads=[Sf_b[h]], writes=[Sbf_b[h]])
                    for h in range(4):
                        g = ggi % 2; ggi += 1
                        P.op('act', lambda e, h=h, g=g: e.activation(out=sq[g][:, :], in_=pb[h][:, :], func=AF.Square), reads=[otb[h]], writes=[sq_b[g]])
                        P.op('pe', lambda e, g=g: e.matmul(pb[5][:, :], lhsT=ones_bf[:, :], rhs=sq[g][:, :], start=True, stop=True),
                             reads=[sq_b[g]], writes=[aux_b])
                        P.op('act', lambda e: e.activation(out=l1[:, :], in_=pb[5][:, :], func=AF.Ln, bias=epsc[:, 0:1], scale=1.0 / 128.0),
                             reads=[aux_b], writes=[l1_b])
                        P.op('act', lambda e: e.activation(out=rr[:, :], in_=l1[:, :], func=AF.Exp, scale=-0.5), reads=[l1_b], writes=[rr_b])
                        P.op('dve', lambda e, h=h: e.tensor_tensor(out=t1[:, :], in0=pb[h][:, :], in1=rr[:, :], op=ALU.mult),
                             reads=[otb[h], rr_b], writes=[t1_b])
                        P.op('dve', lambda e, h=h, g=g, s=s: e.scalar_tensor_tensor(out=gg[g][:, :], in0=t1[:, :], scalar=ggc[:, h:h + 1], in1=szg[s][:, h, :],
                                                                                 op0=ALU.mult, op1=ALU.mult),
                             reads=[t1_b, in_b[s]], writes=[gg_b[g]])
                        P.dma('pool', ggT_s[h * 128:(h + 1) * 128, cs], gg[g][:, :], reads=[gg_b[g]])
                P.flush()

            with ExitStack() as es:
              if 'C' in phases:
                def sb(name, shape, dt):
                    return es.enter_context(nc.sbuf_tensor(f"{name}_{l}", shape, dt))
                wbr = sb("wbr", [128, 12, D], BF16)
                wout = sb("wout", [128, 8, D], BF16)
                wpl = sb("wpl", [128, 4, 128], BF16)
                wplw = sb("wplw", [128, 4, 128], BF16)
                wpln = sb("wpln", [128, 4, 128], BF16)
                psc = sb("psc", [128, 4], F32)
                with ExitStack() as es2:
                    wst = [es2.enter_context(nc.sbuf_tensor(f"cwst{i}_{l}", [128, 4, D], F32)) for i in range(2)]
                    wp32 = es2.enter_context(nc.sbuf_tensor(f"wp32_{l}", [128, 4, 128], F32))
                    wst_b = P.bufs_n(2)
                    srcs = [(w_brs[0][l].rearrange("(k p) n -> p k n", p=128), wbr, 0),
                            (w_brs[1][l].rearrange("(k p) n -> p k n", p=128), wbr, 4),
                            (w_brs[2][l].rearrange("(k p) n -> p k n", p=128), wbr, 8),
                            (w_out[l].rearrange("(k p) n -> p k n", p=128)[:, 0:4, :], wout, 0),
                            (w_out[l].rearrange("(k p) n -> p k n", p=128)[:, 4:8, :], wout, 4)]
                    for i, (src, dstt, k0) in enumerate(srcs):
                        s = i % 2
                        P.dma('sp', wst[s][:], src, writes=[wst_b[s]])
                        P.op('pool', lambda e, s=s, dstt=dstt, k0=k0: e.tensor_copy(out=dstt[:, k0:k0 + 4, :], in_=wst[s][:]),
                             reads=[wst_b[s]])
                    d_wp = P.dma('sp', wp32[:], w_pool[l].rearrange("g c d -> c g d"))
                    P.op('pool', lambda e: e.tensor_copy(out=wpl[:], in_=wp32[:]), deps=[d_wp])
                    for g in range(4):
                        P.op('pool', lambda e, g=g: e.tensor_scalar(out=wplw[:, g, :], in0=wp32[:, g, :], scalar1=1.0 / (2, 4, 8, 16)[g], scalar2=None, op0=ALU.mult),
                             deps=[d_wp])
                    P.op('pool', lambda e: e.tensor_scalar(out=wpln[:], in0=wp32[:], scalar1=-1.0, scalar2=None, op0=ALU.mult), deps=[d_wp])
                    P.dma('sp', psc[:], pscale_col[l])
                    P.flush()
                pu = [sb(f"pu{i}", [128, 4, 528], BF16) for i in range(2)]
                szp = [sb(f"szp{i}", [128, 4, 512], BF16) for i in range(2)]
                gfi = [sb(f"gfi{i}", [128, 4, 512], BF16) for i in range(2)]
                ggi_ = [sb(f"ggi{i}", [128, 4, 512], BF16) for i in range(2)]
                sgm = [sb(f"sgm{i}", [128, 3, 512], BF16) for i in range(3)]
                a2 = sb("a2", [128, 3, 32], F32)
                a4 = sb("a4", [128, 2, 32], F32)
                a8 = sb("a8", [128, 32], F32)
                ssum = sb("ssum", [128, 4, 16], F32)
                dl16 = sb("dl16", [128, 4, 16], BF16)
                gp = sb("gp", [128, 4, 512], BF16)
                tt = [[sb(f"tt{i}_{j}", [128, 512], F32) for j in range(3)] for i in range(2)]
                mg = sb("mg", [128, 8, 512], BF16)
                xin = [sb(f"cxin{i}", [128, D], F32) for i in range(2)]
                rr = sb("crr", [128, D], F32)
                yo = [sb(f"yo{i}", [128, D], F32) for i in range(2)]
                sqj = sb("csqj", [128, 512], BF16)
                ss2 = [sb(f"ss2{i}", [128, 2], F32) for i in range(2)]
                sst = sb("sst", [128, 1], F32)
                lnv = sb("clnv", [128, 1], F32)
                rstd = sb("crstd", [128, 1], F32)
                inb = P.bufs_n(2); sgm_b = P.bufs_n(3)
                a2_b = P.buf(); a4_b = P.buf(); a8_b = P.buf(); ssum_b = P.buf(); tf_b = P.buf(); dif_b = P.buf(); gp_b = P.buf()
                tt_b = [P.bufs_n(3) for _ in range(2)]
                mg_b = P.buf(); xin_b = P.bufs_n(2); rr_b = P.buf(); yo_b = P.bufs_n(2); sqj_b = P.buf(); ss2_b = P.bufs_n(2)
                sst_b = P.buf(); lnv_b = P.buf(); rstd_b = P.buf()
                ppb = P.buf()
                yb = P.bufs_n(3)
                ob = P.bufs_n(4)
                puv = puT_s.rearrange("(g c) s -> c g s", c=128)
                szpv = szpT_s.rearrange("(g c) s -> c g s", c=128)
                gfv = gfT_s.rearrange("(k p) s -> p k s", p=128)
                ggv = ggT_s.rearrange("(k p) s -> p k s", p=128)
                sgv = sgmT_s.rearrange("(b j p) s -> p b j s", p=128, j=8)
                WIN = (2, 4, 8, 16)
                yi = 0; sgi = 0; xi = 0; tti = 0
                def c_load(T):
                    s = T % 2
                    cs = slice(T * 512, (T + 1) * 512)
                    if T == 0:
                        P.op('pool', lambda e, s=s: e.memset(pu[s][:, :, 0:16], 0.0), writes=[inb[s]])
                        P.dma('sp', pu[s][:, :, 16:528], puv[:, :, cs], writes=[inb[s]])
                    else:
                        P.dma('sp', pu[s][:, :, :], puv[:, :, T * 512 - 16:(T + 1) * 512], writes=[inb[s]])
                    P.dma('sp', szp[s][:], szpv[:, :, cs], writes=[inb[s]])
                    P.dma('sp', gfi[s][:], gfv[:, :, cs], writes=[inb[s]])
                    P.dma('sp', ggi_[s][:], ggv[:, :, cs], writes=[inb[s]])

                def c_loadx(ix):
                    r0_ = ix * 128
                    P.dma('sp', xin[ix % 2][:], xsrc[r0_:r0_ + 128, :], writes=[xin_b[ix % 2]])

                def c_loadsg(ig):
                    T_, j_ = divmod(ig, 8)
                    P.dma('sp', sgm[ig % 3][:], sgv[:, :, j_, T_ * 512:(T_ + 1) * 512], writes=[sgm_b[ig % 3]])

                c_load(0)
                c_loadx(0)
                c_loadsg(0)
                c_loadsg(1)
                for T in range(NT):
                    s = T % 2
                    cs = slice(T * 512, (T + 1) * 512)
                    if T + 1 < NT:
                        c_load(T + 1)
                    u = pu[s]
                    if T == 0:
                        P.op('pool', lambda e, u=u: e.tensor_tensor(out=a2[:, :, 1:32], in0=u[:, 1:4, 1:32], in1=u[:, 1:4, 0:31], op=ALU.add),
                             reads=[inb[s]], writes=[a2_b])
                        P.op('pool', lambda e, u=u: e.tensor_tensor(out=ssum[:, 0, :], in0=u[:, 0, 16:32], in1=u[:, 0, 15:31], op=ALU.add),
                             reads=[inb[s]], writes=[ssum_b])
                        P.op('pool', lambda e: e.tensor_tensor(out=a4[:, :, 3:32], in0=a2[:, 1:3, 3:32], in1=a2[:, 1:3, 1:30], op=ALU.add),
                             reads=[a2_b], writes=[a4_b])
                        P.op('pool', lambda e: e.tensor_tensor(out=ssum[:, 1, :], in0=a2[:, 0, 16:32], in1=a2[:, 0, 14:30], op=ALU.add),
                             reads=[a2_b], writes=[ssum_b])
                        P.op('pool', lambda e: e.tensor_tensor(out=a8[:, 7:32], in0=a4[:, 1, 7:32], in1=a4[:, 1, 3:28], op=ALU.add),
                             reads=[a4_b], writes=[a8_b])
                        P.op('pool', lambda e: e.tensor_tensor(out=ssum[:, 2, :], in0=a4[:, 0, 16:32], in1=a4[:, 0, 12:28], op=ALU.add),
                             reads=[a4_b], writes=[ssum_b])
                        P.op('pool', lambda e: e.tensor_tensor(out=ssum[:, 3, :], in0=a8[:, 16:32], in1=a8[:, 8:24], op=ALU.add),
                             reads=[a8_b], writes=[ssum_b])
                        P.op('pool', lambda e: e.tensor_tensor(out=dl16[:], in0=ssum[:], in1=rcfix[:], op=ALU.mult), reads=[ssum_b], writes=[dif_b])
                    for g in range(4):
                        wn = WIN[g]
                        for i in range(wn):
                            P.op('pe', lambda e, g=g, i=i, u=u: e.matmul(pb[0][:, :], lhsT=wplw[:, g, :], rhs=u[:, g, 16 - i:528 - i], start=(i == 0), stop=False),
                                 reads=[inb[s]], writes=[ppb])
                        P.op('pe', lambda e, g=g, u=u, T=T: e.matmul(pb[0][:, :], lhsT=wpln[:, g, :], rhs=u[:, g, 16:528], start=False, stop=(T != 0)),
                             reads=[inb[s]], writes=[ppb])
                        if T == 0:
                            P.op('pe', lambda e, g=g: e.matmul(pb[0][:, 0:16], lhsT=wpl[:, g, :], rhs=dl16[:, g, :], start=False, stop=True),
                                 reads=[dif_b], writes=[ppb])
                        P.op('dve', lambda e, g=g, s=s: e.scalar_tensor_tensor(out=gp[:, g, :], in0=pb[0][:, :], scalar=psc[:, g:g + 1], in1=szp[s][:, g, :],
                                                                             op0=ALU.mult, op1=ALU.mult),
                             reads=[ppb, inb[s]], writes=[gp_b])
                    for j in range(8):
                        q = sgi % 3
                        if sgi + 2 < NT * 8:
                            c_loadsg(sgi + 2)
                        sgi += 1
                        srcs3 = [(gfi[s], 0, inb[s]), (ggi_[s], 4, inb[s]), (gp, 8, gp_b)]
                        w_ = tti % 2; tti += 1
                        for br in range(3):
                            a = yi % 3; yi += 1
                            src, k0, rb = srcs3[br]
                            for k in range(4):
                                P.op('pe', lambda e, a=a, k=k, k0=k0, src=src, j=j: e.matmul(
                                    pb[1 + a][:, :], lhsT=wbr[:, k0 + k, j * 128:(j + 1) * 128], rhs=src[:, k, :], start=(k == 0), stop=(k == 3)),
                                    reads=[rb], writes=[yb[a]])
                            P.op('dve', lambda e, a=a, br=br, q=q, w_=w_: e.tensor_tensor(out=tt[w_][br][:, :], in0=pb[1 + a][:, :], in1=sgm[q][:, br, :], op=ALU.mult),
                                 reads=[yb[a], sgm_b[q]], writes=[tt_b[w_][br]])
                        P.op('dve', lambda e, w_=w_: e.tensor_tensor(out=tt[w_][0][:, :], in0=tt[w_][0][:, :], in1=tt[w_][1][:, :], op=ALU.add),
                             reads=[tt_b[w_][1]], writes=[tt_b[w_][0]])
                        P.op('pool', lambda e, w_=w_, j=j: e.tensor_tensor(out=mg[:, j, :], in0=tt[w_][0][:, :], in1=tt[w_][2][:, :], op=ALU.add),
                             reads=[tt_b[w_][0], tt_b[w_][2]], writes=[mg_b])
                    for tb in range(4):
                        xs = xi % 2
                        if xi + 1 < NT * 4:
                            c_loadx(xi + 1)
                        xi += 1
                        r0 = T * 512 + tb * 128
                        P.op('pool', lambda e, xs=xs: e.memset(ss2[xs][:], 0.0), writes=[ss2_b[xs]])
                        op_ = 2 * (tb % 2)
                        for n in range(2):
                            for k in range(8):
                                P.op('pe', lambda e, n=n, k=k, tb=tb, op_=op_: e.matmul(pb[4 + op_ + n][:, :], lhsT=mg[:, k, tb * 128:(tb + 1) * 128],
                                                                               rhs=wout[:, k, n * 512:(n + 1) * 512], start=(k == 0), stop=(k == 7)),
                                     reads=[mg_b], writes=[ob[op_ + n]])
                            P.op('act', lambda e, n=n, xs=xs, op_=op_: e.activation(out=sqj[:, :], in_=pb[4 + op_ + n][:, :], func=AF.Square, accum_out=ss2[xs][:, n:n + 1]),
                                 reads=[ob[op_ + n]], writes=[sqj_b, ss2_b[xs]])
                        P.op('dve', lambda e, xs=xs: e.tensor_tensor(out=sst[:], in0=ss2[xs][:, 0:1], in1=ss2[xs][:, 1:2], op=ALU.add),
                             reads=[ss2_b[xs]], writes=[sst_b])
                        P.op('act', lambda e: e.activation(out=lnv[:], in_=sst[:], func=AF.Ln, bias=epsc[:, 0:1], scale=1.0 / D), reads=[sst_b], writes=[lnv_b])
                        P.op('act', lambda e: e.activation(out=rstd[:], in_=lnv[:], func=AF.Exp, scale=-0.5), reads=[lnv_b], writes=[rstd_b])
                        for n in range(2):
                            P.op('dve', lambda e, n=n, op_=op_: e.scalar_tensor_tensor(out=rr[:, n * 512:(n + 1) * 512], in0=pb[4 + op_ + n][:, :], scalar=rstd[:, 0:1],
                                                                              in1=G2[:, n * 512:(n + 1) * 512], op0=ALU.mult, op1=ALU.mult),
                                 reads=[ob[op_ + n], rstd_b], writes=[rr_b])
                        P.op('pool', lambda e, xs=xs: e.tensor_tensor(out=yo[xs][:], in0=rr[:], in1=xin[xs][:], op=ALU.add),
                             reads=[rr_b, xin_b[xs]], writes=[yo_b[xs]])
                        P.dma('pool', xdst[r0:r0 + 128, :], yo[xs][:], reads=[yo_b[xs]])
                P.flush()
        stats = (P.ninst, P.nwaits)
    return nc, stats


def _consts():
    bf = ml_dtypes.bfloat16
    ident = np.eye(128, dtype=np.float32).astype(bf)
    k = np.arange(128)[:, None, None]
    j = np.arange(4)[None, :, None]
    q = np.arange(512)[None, None, :]
    maskA = np.where(128 * j + k <= q, 0.0, NEG).astype(np.float32).astype(bf)
    s_ = np.arange(128)[:, None, None]
    t_ = np.arange(128)[None, None, :]
    cm4 = np.broadcast_to((s_ <= t_).astype(np.float32), (128, 4, 128)).astype(bf)
    segm = np.ones((64, 512), np.float32)
    segm[:, ::128] = 0.0
    rcfix = np.zeros((128, 4, 16), np.float32)
    for g, w in enumerate((2, 4, 8, 16)):
        rcfix[:, g, :] = 1.0 / np.minimum(np.arange(1, 17), w) - 1.0 / w
    return dict(c_ident=ident, c_maskA=np.ascontiguousarray(maskA), c_cm4=np.ascontiguousarray(cm4), c_segm=segm, c_rcfix=rcfix)


def make_in_maps(inputs, S, DEPTH, ncores):
    f = lambda a: np.ascontiguousarray(np.asarray(a, dtype=np.float32))
    x = f(inputs['x']); c = f(inputs['c'])
    shared = dict(
        w_ada=f(inputs['w_ada'])[:DEPTH],
        b_ada_col=np.ascontiguousarray(f(inputs['b_ada'])[:DEPTH].reshape(DEPTH, 24, 128).transpose(0, 2, 1)),
        b_ada_row=np.ascontiguousarray(f(inputs['b_ada'])[:DEPTH].reshape(DEPTH, 1, 3 * D)),
        g_pre_col=np.ascontiguousarray(f(inputs['g_pre'])[:DEPTH].reshape(DEPTH, 8, 128).transpose(0, 2, 1)),
        g_post_row=np.ascontiguousarray(f(inputs['g_post'])[:DEPTH].reshape(DEPTH, 1, D)),
        w_in=f(inputs['w_in'])[:DEPTH],
        b_forget_col=np.ascontiguousarray(f(inputs['b_forget'])[:DEPTH].reshape(DEPTH, 8, 1)),
        w_gla_gate=f(inputs['w_gla_gate'])[:DEPTH],
        b_gla_col=np.ascontiguousarray(f(inputs['b_gla_gate'])[:DEPTH].reshape(DEPTH, 4, 64).transpose(0, 2, 1)),
        g_gla_col=np.ascontiguousarray(f(inputs['g_gla_norm'])[:DEPTH].reshape(DEPTH, 4, 128).transpose(0, 2, 1)),
        w_pool=f(inputs['w_pool'])[:DEPTH],
        pscale_col=np.ascontiguousarray(f(inputs['pool_scale'])[:DEPTH].reshape(DEPTH, 4, 128).transpose(0, 2, 1)),
        w_br_fox=f(inputs['w_br_fox'])[:DEPTH], w_br_gla=f(inputs['w_br_gla'])[:DEPTH], w_br_pool=f(inputs['w_br_pool'])[:DEPTH],
        w_out=f(inputs['w_out'])[:DEPTH],
    )
    shared.update(_consts())
    maps = []
    for b in range(ncores):
        m = dict(shared)
        m['x'] = np.ascontiguousarray(x[b])
        m['ct'] = np.ascontiguousarray(c[b].reshape(8, 128).T)
        maps.append(m)
    return maps


_CACHE = {}


def kernel(**inputs):
    x = np.asarray(inputs['x'])
    B, S, _ = x.shape
    DEPTH = np.asarray(inputs['w_in']).shape[0]
    key = (S, DEPTH)
    if key not in _CACHE:
        _CACHE[key] = build(S, DEPTH)[0]
    nc = _CACHE[key]
    maps = make_in_maps(inputs, S, DEPTH, B)
    res = run_bass_kernel_spmd(nc, maps, core_ids=list(range(B)))
    return np.stack([np.asarray(r["y"], dtype=np.float32) for r in res.results], axis=0)
```

```python
import numpy as np
import ml_dtypes
from contextlib import ExitStack
import concourse.bass as bass
import concourse.mybir as mybir
from concourse.bass_utils import run_bass_kernel_spmd

F32 = mybir.dt.float32
BF16 = mybir.dt.bfloat16
AF = mybir.ActivationFunctionType
ALU = mybir.AluOpType

ENGS = ['pe', 'act', 'dve', 'pool', 'sp']
SEG = 30000
NSEG = 8
NDSEM = 12
D = 1024
IN_W = 7704
EPS = 1e-6
NEG = -30000.0

SEGS = [
    ('fq', 0, 512, 'f', 'q'), ('fk', 512, 512, 'f', 'copy'), ('fv', 1024, 512, 't', 'copy'),
    ('ff', 1536, 8, 'ff', None),
    ('gq', 1544, 256, 'f', 'copy'), ('gk', 1800, 256, 'f', 'copy'), ('gv', 2056, 512, 't', 'copy'),
    ('glr', 2568, 16, 'f', 'copy'), ('pu', 2584, 512, 'f', 'copy'),
    ('zf', 3096, 512, 'f', 'silu'), ('zg', 3608, 512, 'f', 'silu'), ('zp', 4120, 512, 'f', 'silu'),
    ('mg', 4632, 3072, 'f', 'sigmoid'),
]


class Buf:
    __slots__ = ('wl', 'r', 'gdeps')

    def __init__(self):
        self.wl = []
        self.r = []
        self.gdeps = []


class Op:
    __slots__ = ('eng', 'fn', 'deps', 'inc', 'cnt', 'seg', 'is_dma', 'dsem', 'dcnt', 'gen')


class Prog:
    def __init__(self, nc, es):
        self.nc = nc
        self.q = {e: [] for e in ENGS}
        self.csem = {e: [es.enter_context(nc.semaphore(f"c_{e}_{i}")) for i in range(NSEG)] for e in ENGS if e != 'sp'}
        self.dsem = {e: [es.enter_context(nc.semaphore(f"d_{e}_{i}")) for i in range(NDSEM)] for e in ('sp', 'pool')}
        self.ccount = {e: 0 for e in ENGS}
        self.dcount = {e: 0 for e in ENGS}
        self.waited = {e: {} for e in ENGS}
        self.bufs = []
        self.nwaits = 0
        self.ninst = 0
        self.gen = 0

    def buf(self):
        b = Buf()
        self.bufs.append(b)
        return b

    def bufs_n(self, n):
        return [self.buf() for _ in range(n)]

    def op(self, eng, fn, reads=(), writes=(), deps=(), is_dma=False):
        o = Op()
        o.eng = eng
        o.fn = fn
        o.inc = False
        o.is_dma = is_dma
        o.cnt = None
        d = [x for x in deps if x is not None]
        for b in reads:
            d.extend(b.wl)
        join = []
        newg = []
        for b in writes:
            if is_dma and b.wl and not b.r and all(w.is_dma for w in b.wl):
                d.extend(b.gdeps)
                join.append(b)
            else:
                g = list(b.wl)
                last = {}
                for r in b.r:
                    if r.is_dma:
                        g.append(r)
                    else:
                        last[r.eng] = r
                g.extend(last.values())
                d.extend(g)
                newg.append((b, g))
        for b in reads:
            b.r.append(o)
        for b in join:
            b.wl.append(o)
        for b, g in newg:
            b.wl = [o]
            b.r = []
            b.gdeps = g
        o.gen = self.gen
        o.deps = [x for x in d if x.gen == self.gen and not (x.eng == 'pe' and eng == 'pe' and not x.is_dma)]
        self.q[eng].append(o)
        return o

    def dma(self, eng, out, in_, reads=(), writes=(), deps=()):
        return self.op(eng, lambda e: e.dma_start(out=out, in_=in_), reads, writes, deps, is_dma=True)

    def flush(self):
        nc = self.nc
        alld = [o for e in ENGS for o in self.q[e] if o.is_dma]
        if alld:
            self.op('sp', lambda e: e.nop(), deps=alld)
        for e in ENGS:
            for o in self.q[e]:
                for d in o.deps:
                    if not d.is_dma:
                        d.inc = True
        for e in ENGS:
            for o in self.q[e]:
                if o.is_dma:
                    k = self.dcount[e]
                    o.dsem = self.dsem[e][k % NDSEM]
                    o.dcnt = 16 * (k // NDSEM + 1)
                    self.dcount[e] = k + 1
                elif o.inc:
                    n = self.ccount[e]
                    o.seg = n // SEG
                    o.cnt = n % SEG + 1
                    self.ccount[e] = n + 1
                    assert o.seg < NSEG
        csem = self.csem

        def run(e, eng):
            waited = self.waited[e]
            for o in self.q[e]:
                need = {}
                for d in o.deps:
                    if d.is_dma:
                        key, val = d.dsem, d.dcnt
                    else:
                        key, val = csem[d.eng][d.seg], d.cnt
                    if need.get(key, 0) < val:
                        need[key] = val
                if o.is_dma and o.dcnt > 16:
                    if need.get(o.dsem, 0) < o.dcnt - 16:
                        need[o.dsem] = o.dcnt - 16
                for key, val in need.items():
                    if waited.get(key, 0) >= val:
                        continue
                    eng.wait_ge(key, val)
                    waited[key] = val
                    self.nwaits += 1
                ins = o.fn(eng)
                self.ninst += 1
                if o.is_dma:
                    ins.then_inc(o.dsem, 16)
                elif o.inc:
                    ins.then_inc(csem[e][o.seg], 1)

        with nc.Block() as block:
            @block.tensor
            def _(eng):
                run('pe', eng)

            @block.scalar
            def _(eng):
                run('act', eng)

            @block.vector
            def _(eng):
                run('dve', eng)

            @block.gpsimd
            def _(eng):
                run('pool', eng)

            @block.sync
            def _(eng):
                run('sp', eng)
        nc.all_engine_barrier()
        self.gen += 1
        self.q = {e: [] for e in ENGS}
        for b in self.bufs:
            b.wl = []
            b.r = []
            b.gdeps = []
        self.bufs = []


def build(S, DEPTH, dbg=(), phases=('M', 'A', 'B1', 'B2', 'C')):
    NT = S // 512
    NB = S // 128
    SUP = min(S, 4096)
    NSUP = S // SUP
    nc = bass.Bass("TRN2", target_bir_lowering=False)

    def din(name, shape, dt=F32):
        return nc.dram_tensor(name, shape, dt, kind="ExternalInput").ap()

    def dscr(name, shape, dt=BF16):
        kind = "ExternalOutput" if name in dbg else "Internal"
        return nc.dram_tensor(name, shape, dt, kind=kind).ap()

    x = din("x", [S, D])
    ct = din("ct", [128, 8])
    w_ada = din("w_ada", [DEPTH, D, 3 * D])
    b_ada_col = din("b_ada_col", [DEPTH, 128, 24])
    b_ada_row = din("b_ada_row", [DEPTH, 1, 3 * D])
    g_pre_col = din("g_pre_col", [DEPTH, 128, 8])
    g_post_row = din("g_post_row", [DEPTH, 1, D])
    w_in = din("w_in", [DEPTH, D, IN_W])
    b_forget_col = din("b_forget_col", [DEPTH, 8, 1])
    w_gg = din("w_gla_gate", [DEPTH, 16, 256])
    b_gla_col = din("b_gla_col", [DEPTH, 64, 4])
    g_gla_col = din("g_gla_col", [DEPTH, 128, 4])
    w_pool = din("w_pool", [DEPTH, 4, 128, 128])
    pscale_col = din("pscale_col", [DEPTH, 128, 4])
    w_brs = [din(n, [DEPTH, 512, D]) for n in ("w_br_fox", "w_br_gla", "w_br_pool")]
    w_out = din("w_out", [DEPTH, D, D])
    c_ident = din("c_ident", [128, 128], BF16)
    c_maskA = din("c_maskA", [128, 4, 512], BF16)
    c_cm4 = din("c_cm4", [128, 4, 128], BF16)
    c_segm = din("c_segm", [64, 512])
    c_rcfix = din("c_rcfix", [128, 4, 16])
    y = nc.dram_tensor("y", [S, D], F32, kind="ExternalOutput").ap()

    qT_s = dscr("qT_s", [512, S]); kT_s = dscr("kT_s", [512, S]); v_s = dscr("v_s", [S, 512])
    augq_s = dscr("augq_s", [8, 3, S]); augk_s = dscr("augk_s", [8, 3, S])
    gqT_s = dscr("gqT_s", [256, S]); gkT_s = dscr("gkT_s", [256, S]); gv_s = dscr("gv_s", [S, 512])
    glrT_s = dscr("glrT_s", [16, S]); puT_s = dscr("puT_s", [512, S])
    szfT_s = dscr("szfT_s", [512, S]); szgT_s = dscr("szgT_s", [512, S]); szpT_s = dscr("szpT_s", [512, S])
    sgmT_s = dscr("sgmT_s", [3072, S])
    gfT_s = dscr("gfT_s", [512, S]); ggT_s = dscr("ggT_s", [512, S])
    xbuf = [dscr("xbuf0", [S, D], F32), dscr("xbuf1", [S, D], F32)]
    seg_dst = {'fq': qT_s, 'fk': kT_s, 'fv': v_s, 'gq': gqT_s, 'gk': gkT_s, 'gv': gv_s, 'glr': glrT_s,
               'pu': puT_s, 'zf': szfT_s, 'zg': szgT_s, 'zp': szpT_s, 'mg': sgmT_s}

    with ExitStack() as ges:
        P = Prog(nc, ges)

        def gsb(name, shape, dt):
            return ges.enter_context(nc.sbuf_tensor(name, shape, dt))

        pbig = [ges.enter_context(nc.psum_tensor(f"pbig{i}", [128, 1024], F32)) for i in range(4)]
        pb = [pbig[i // 2][:, (i % 2) * 512:(i % 2 + 1) * 512] for i in range(8)]
        ptv = pbig[3].bitcast(BF16)
        ptbs = [ptv[:, 0:1024], ptv[:, 1024:2048]]
        ptb = ptbs[0]
        ident = gsb("ident", [128, 128], BF16)
        maskA = gsb("maskA", [128, 4, 512], BF16)
        cm4 = gsb("cm4", [128, 4, 128], BF16)
        segm = gsb("segm", [64, 512], F32)
        rcfix = gsb("rcfix", [128, 4, 16], F32)
        ones_f = gsb("ones_f", [128, 512], F32)
        ones_bf = gsb("ones_bf", [128, 128], BF16)
        epsc = gsb("epsc", [128, 1], F32)
        onec = gsb("onec", [128, 1], F32)
        sc = gsb("sc", [128, 8], F32)
        scb = gsb("scb", [128, 8, 128], F32)
        ctt = gsb("ctt", [128, 8], F32)
        Acol = gsb("Acol", [128, 8], F32)
        Bcol = gsb("Bcol", [128, 8], F32)
        G2 = gsb("G2", [128, D], F32)
        bacol = gsb("bacol", [128, 24], F32)
        gprec = gsb("gprec", [128, 8], F32)
        tmp8 = gsb("tmp8", [128, 8], F32)

        P.dma('sp', ident[:], c_ident)
        P.dma('sp', maskA[:], c_maskA)
        P.dma('sp', cm4[:], c_cm4)
        P.dma('sp', segm[:], c_segm)
        P.dma('sp', rcfix[:], c_rcfix)
        d_ct = P.dma('sp', ctt[:], ct)
        P.op('pool', lambda e: e.memset(ones_f[:], 1.0))
        P.op('pool', lambda e: e.memset(ones_bf[:], 1.0))
        P.op('pool', lambda e: e.memset(epsc[:], EPS))
        P.op('pool', lambda e: e.memset(onec[:], 1.0))
        o_sc = P.op('act', lambda e: e.activation(out=sc[:], in_=ctt[:], func=AF.Silu), deps=[d_ct])
        P.flush()
        for k in range(8):
            P.op('dve', lambda e, k=k: e.tensor_scalar(out=scb[:, k, :], in0=ones_f[:, 0:128], scalar1=sc[:, k:k + 1],
                                                       scalar2=None, op0=ALU.mult))
        P.flush()

        for l in range(DEPTH):
            xsrc = x if l == 0 else xbuf[(l - 1) % 2]
            xdst = y if l == DEPTH - 1 else xbuf[l % 2]

            with ExitStack() as es:
              if 'M' in phases:
                def sb(name, shape, dt):
                    return es.enter_context(nc.sbuf_tensor(f"{name}_{l}", shape, dt))
                wa = [sb(f"wa{i}", [128, 8, 512], F32) for i in range(2)]
                wa_b = P.bufs_n(2)
                gpb = sb("gpb", [128, D], F32)
                bgb = sb("bgb", [128, D], F32)
                modps = pb[0]
                b_mod = P.buf()
                gps = [pb[1], pb[2]]
                b_gps = P.bufs_n(2)
                wav = w_ada[l].rearrange("(k p) n -> p k n", p=128)
                d1 = P.dma('pool', bacol[:], b_ada_col[l])
                d2 = P.dma('pool', gprec[:], g_pre_col[l])
                d3 = P.dma('pool', gpb[:], g_post_row[l].partition_broadcast(128))
                d4 = P.dma('pool', bgb[:], b_ada_row[l][:, 2 * D:3 * D].partition_broadcast(128))
                for nb in range(6):
                    s = nb % 2
                    P.dma('sp', wa[s][:], wav[:, :, nb * 512:(nb + 1) * 512], writes=[wa_b[s]])
                    if nb < 4:
                        for j in range(4):
                            ci = nb * 4 + j
                            for k in range(8):
                                P.op('pe', lambda e, s=s, j=j, k=k, ci=ci: e.matmul(
                                    modps[:, ci:ci + 1], lhsT=wa[s][:, k, j * 128:(j + 1) * 128], rhs=sc[:, k:k + 1],
                                    start=(k == 0), stop=(k == 7)), reads=[wa_b[s]], writes=[b_mod])
                    else:
                        hf = nb - 4
                        for k in range(8):
                            P.op('pe', lambda e, s=s, k=k, hf=hf: e.matmul(
                                gps[hf][:, :], lhsT=scb[:, k, :], rhs=wa[s][:, k, :], start=(k == 0), stop=(k == 7)),
                                reads=[wa_b[s]], writes=[b_gps[hf]])
                P.op('dve', lambda e: e.tensor_tensor(out=Bcol[:], in0=modps[:, 0:8], in1=bacol[:, 0:8], op=ALU.add),
                     reads=[b_mod], deps=[d1])
                o_t8 = P.op('dve', lambda e: e.tensor_tensor(out=tmp8[:], in0=modps[:, 8:16], in1=bacol[:, 8:16], op=ALU.add),
                            reads=[b_mod], deps=[d1])
                P.op('dve', lambda e: e.scalar_tensor_tensor(out=Acol[:], in0=tmp8[:], scalar=1.0, in1=gprec[:],
                                                             op0=ALU.add, op1=ALU.mult), deps=[o_t8, d2])
                for hf in range(2):
                    o1 = P.op('dve', lambda e, hf=hf: e.tensor_tensor(out=G2[:, hf * 512:(hf + 1) * 512], in0=gps[hf][:, :],
                                                                      in1=bgb[:, hf * 512:(hf + 1) * 512], op=ALU.add),
                              reads=[b_gps[hf]], deps=[d4])
                    P.op('dve', lambda e, hf=hf: e.tensor_tensor(out=G2[:, hf * 512:(hf + 1) * 512],
                                                                 in0=G2[:, hf * 512:(hf + 1) * 512],
                                                                 in1=gpb[:, hf * 512:(hf + 1) * 512], op=ALU.mult),
                         deps=[o1, d3])
                P.flush()

            with ExitStack() as es:
              if 'A' in phases:
                def sb(name, shape, dt):
                    return es.enter_context(nc.sbuf_tensor(f"{name}_{l}", shape, dt))
                hT = sb("hT", [128, 8, SUP], BF16)
                wst = [sb(f"wst{i}", [128, 8, 512], F32) for i in range(2)]
                wbf = [sb(f"wbf{i}", [128, 8, 512], BF16) for i in range(2)]
                xin = [sb(f"xin{i}", [128, D], F32) for i in range(3)]
                xn = [sb(f"xn{i}", [128, D], BF16) for i in range(2)]
                sqj = sb("sqj", [128, D], BF16)
                ssq = [sb(f"ssq{i}", [128, 1], F32) for i in range(2)]
                lnv = [sb(f"lnv{i}", [128, 1], F32) for i in range(2)]
                rstd = [sb(f"rstd{i}", [128, 1], F32) for i in range(2)]
                stage = [sb(f"stage{i}", [128, 512], BF16) for i in range(4)]
                nbf = sb("nbf", [8, 1], F32)
                bfc = sb("bfc", [8, 1], F32)
                ft1 = sb("ft1", [8, 512], F32)
                ft2 = sb("ft2", [8, 512], F32)
                Fneg = [sb(f"Fneg{i}", [8, 512], F32) for i in range(2)]
                r1 = sb("r1", [8, 512], F32)
                r2 = sb("r2", [8, 512], F32)
                hk = [sb(f"hk{i}", [8, 3, 512], BF16) for i in range(2)]
                hq = [sb(f"hq{i}", [8, 3, 512], BF16) for i in range(2)]
                d_bf = P.dma('pool', bfc[:], b_forget_col[l])
                P.op('pool', lambda e: e.tensor_scalar(out=nbf[:], in0=bfc[:], scalar1=-1.0, scalar2=None, op0=ALU.mult),
                     deps=[d_bf])
                P.flush()
                winv = w_in[l].rearrange("(k p) n -> p k n", p=128)
                fcount = 0
                for sup in range(NSUP):
                    t0 = sup * SUP
                    xin_b = P.bufs_n(3); xn_b = P.bufs_n(2); ss_b = P.bufs_n(2); ln_b = P.bufs_n(2); rs_b = P.bufs_n(2)
                    tp_bs = P.bufs_n(2); sqj_b = P.buf()
                    for tb in range(0 if 'noA1' in phases else SUP // 128):
                        s = tb % 2
                        x3 = tb % 3
                        if tb == 0:
                            P.dma('sp', xin[0][:], xsrc[t0:t0 + 128, :], writes=[xin_b[0]])
                        if tb + 1 < SUP // 128:
                            r1_ = t0 + (tb + 1) * 128
                            P.dma('sp', xin[(tb + 1) % 3][:], xsrc[r1_:r1_ + 128, :], writes=[xin_b[(tb + 1) % 3]])
                        P.op('pool', lambda e, s=s: e.memset(ssq[s][:], 0.0), writes=[ss_b[s]])
                        P.op('act', lambda e, s=s, x3=x3: e.activation(out=sqj[:], in_=xin[x3][:], func=AF.Square, accum_out=ssq[s][:, 0:1]),
                             reads=[xin_b[x3]], writes=[sqj_b, ss_b[s]])
                        P.op('act', lambda e, s=s: e.activation(out=lnv[s][:], in_=ssq[s][:], func=AF.Ln, bias=epsc[:, 0:1], scale=1.0 / D),
                             reads=[ss_b[s]], writes=[ln_b[s]])
                        P.op('act', lambda e, s=s: e.activation(out=rstd[s][:], in_=lnv[s][:], func=AF.Exp, scale=-0.5),
                             reads=[ln_b[s]], writes=[rs_b[s]])
                        P.op('dve', lambda e, s=s, x3=x3: e.tensor_scalar(out=xn[s][:], in0=xin[x3][:], scalar1=rstd[s][:, 0:1], scalar2=None,
                                                                   op0=ALU.mult), reads=[xin_b[x3], rs_b[s]], writes=[xn_b[s]])
                        tp_b = tp_bs[s]
                        for k in range(8):
                            P.op('pe', lambda e, s=s, k=k: e.transpose(ptbs[s][:, k * 128:(k + 1) * 128], xn[s][:, k * 128:(k + 1) * 128], ident[:]),
                                 reads=[xn_b[s]], writes=[tp_b])
                        for k in range(8):
                            dst = hT[:, k, tb * 128:(tb + 1) * 128]
                            src = ptbs[s][:, k * 128:(k + 1) * 128]
                            if s == 0:
                                P.op('act', lambda e, dst=dst, src=src, k=k: e.activation(out=dst, in_=src, func=AF.Identity,
                                                                                        bias=Bcol[:, k:k + 1], scale=Acol[:, k:k + 1]),
                                     reads=[tp_b])
                            else:
                                P.op('dve', lambda e, dst=dst, src=src, k=k: e.tensor_scalar(out=dst, in0=src, scalar1=Acol[:, k:k + 1],
                                                                                           scalar2=Bcol[:, k:k + 1], op0=ALU.mult, op1=ALU.add),
                                     reads=[tp_b])
                    P.flush()
                    wst_b = P.bufs_n(2); wbf_b = P.bufs_n(2); st_b = P.bufs_n(4); acc_b = P.bufs_n(6)
                    ft1_b = P.buf(); ft2_b = P.buf(); F_b = P.bufs_n(2); r1_b = P.buf(); r2_b = P.buf()
                    hk_b = P.bufs_n(2); hq_b = P.bufs_n(2)
                    blk = 0
                    acc_i = 0
                    st_i = 0
                    segsel = [sg for sg in SEGS if not any(p.startswith('seg:') for p in phases) or ('seg:' + sg[0]) in phases]
                    for (name, c0, width, layout, evac) in segsel:
                        for off in range(0, width, 512):
                            w = min(512, width - off)
                            s = blk % 2
                            blk += 1
                            P.dma('pool', wst[s][:, :, 0:w], winv[:, :, c0 + off:c0 + off + w], writes=[wst_b[s]])
                            P.op('pool', lambda e, s=s, w=w: e.tensor_copy(out=wbf[s][:, :, 0:w], in_=wst[s][:, :, 0:w]),
                                 reads=[wst_b[s]], writes=[wbf_b[s]])
                            if layout == 'f':
                                dst = seg_dst[name]
                                for j in range((w + 127) // 128):
                                    M = min(128, w - j * 128)
                                    row0 = off + j * 128
                                    for t in range(SUP // 512):
                                        a = acc_i % 6; acc_i += 1
                                        for k in range(8):
                                            P.op('pe', lambda e, a=a, s=s, k=k, j=j, M=M, t=t: e.matmul(
                                                pb[a][0:M, :], lhsT=wbf[s][:, k, j * 128:j * 128 + M], rhs=hT[:, k, t * 512:(t + 1) * 512],
                                                start=(k == 0), stop=(k == 7)), reads=[wbf_b[s]], writes=[acc_b[a]])
                                        g = st_i % 4; st_i += 1
                                        if evac == 'silu' or evac == 'sigmoid':
                                            fn = AF.Silu if evac == 'silu' else AF.Sigmoid
                                            P.op('act', lambda e, a=a, g=g, M=M, fn=fn: e.activation(out=stage[g][0:M, :], in_=pb[a][0:M, :], func=fn),
                                                 reads=[acc_b[a]], writes=[st_b[g]])
                                        elif evac == 'q':
                                            P.op('dve', lambda e, a=a, g=g, M=M: e.tensor_scalar(out=stage[g][0:M, :], in0=pb[a][0:M, :], scalar1=0.125,
                                                                                                 scalar2=None, op0=ALU.mult),
                                                 reads=[acc_b[a]], writes=[st_b[g]])
                                        else:
                                            P.op('dve', lambda e, a=a, g=g, M=M: e.tensor_copy(out=stage[g][0:M, :], in_=pb[a][0:M, :]),
                                                 reads=[acc_b[a]], writes=[st_b[g]])
                                        P.dma('sp', dst[row0:row0 + M, t0 + t * 512:t0 + (t + 1) * 512], stage[g][0:M, :], reads=[st_b[g]])
                            elif layout == 't':
                                dst = seg_dst[name]
                                for tb in range(SUP // 128):
                                    a = acc_i % 6; acc_i += 1
                                    for k in range(8):
                                        P.op('pe', lambda e, a=a, s=s, k=k, tb=tb: e.matmul(
                                            pb[a][:, :], lhsT=hT[:, k, tb * 128:(tb + 1) * 128], rhs=wbf[s][:, k, 0:512],
                                            start=(k == 0), stop=(k == 7)), reads=[wbf_b[s]], writes=[acc_b[a]])
                                    g = st_i % 4; st_i += 1
                                    P.op('dve', lambda e, a=a, g=g: e.tensor_copy(out=stage[g][:, :], in_=pb[a][:, :]),
                                         reads=[acc_b[a]], writes=[st_b[g]])
                                    P.dma('sp', dst[t0 + tb * 128:t0 + (tb + 1) * 128, :], stage[g][:, :], reads=[st_b[g]])
                            else:
                                for t in range(SUP // 512):
                                    a = acc_i % 6; acc_i += 1
                                    for k in range(8):
                                        P.op('pe', lambda e, a=a, s=s, k=k, t=t: e.matmul(
                                            pb[a][0:8, :], lhsT=wbf[s][:, k, 0:8], rhs=hT[:, k, t * 512:(t + 1) * 512],
                                            start=(k == 0), stop=(k == 7)), reads=[wbf_b[s]], writes=[acc_b[a]])
                                    f = fcount % 2
                                    P.op('act', lambda e, a=a: e.activation(out=ft1[:], in_=pb[a][0:8, :], func=AF.Exp, bias=nbf[:, 0:1], scale=-1.0),
                                         reads=[acc_b[a]], writes=[ft1_b])
                                    P.op('act', lambda e: e.activation(out=ft2[:], in_=ft1[:], func=AF.Ln, bias=onec[0:8, 0:1], scale=1.0),
                                         reads=[ft1_b], writes=[ft2_b])
                                    if fcount == 0:
                                        P.op('dve', lambda e, f=f: e.tensor_tensor_scan(out=Fneg[f][:], data0=ones_f[0:8, :], data1=ft2[:], initial=0.0,
                                                                                        op0=ALU.mult, op1=ALU.add), reads=[ft2_b], writes=[F_b[f]])
                                    else:
                                        P.op('dve', lambda e, f=f: e.tensor_tensor_scan(out=Fneg[f][:], data0=ones_f[0:8, :], data1=ft2[:],
                                                                                        initial=Fneg[1 - f][:, 511:512], op0=ALU.mult, op1=ALU.add),
                                             reads=[ft2_b, F_b[1 - f]], writes=[F_b[f]])
                                    P.op('dve', lambda e, f=f: e.tensor_copy(out=hk[f][:, 0, :], in_=Fneg[f][:]), reads=[F_b[f]], writes=[hk_b[f]])
                                    P.op('dve', lambda e, f=f: e.tensor_tensor(out=r1[:], in0=Fneg[f][:], in1=hk[f][:, 0, :], op=ALU.subtract),
                                         reads=[F_b[f], hk_b[f]], writes=[r1_b])
                                    P.op('dve', lambda e, f=f: e.tensor_copy(out=hk[f][:, 1, :], in_=r1[:]), reads=[r1_b], writes=[hk_b[f]])
                                    P.op('dve', lambda e, f=f: e.tensor_tensor(out=r2[:], in0=r1[:], in1=hk[f][:, 1, :], op=ALU.subtract),
                                         reads=[r1_b, hk_b[f]], writes=[r2_b])
                                    P.op('dve', lambda e, f=f: e.tensor_copy(out=hk[f][:, 2, :], in_=r2[:]), reads=[r2_b], writes=[hk_b[f]])
                                    P.op('dve', lambda e, f=f: e.tensor_scalar(out=hq[f][:], in0=hk[f][:], scalar1=-1.0, scalar2=None, op0=ALU.mult),
                                         reads=[hk_b[f]], writes=[hq_b[f]])
                                    cs = slice(t0 + t * 512, t0 + (t + 1) * 512)
                                    P.dma('sp', augk_s[:, :, cs], hk[f][:], reads=[hk_b[f]])
                                    P.dma('sp', augq_s[:, :, cs], hq[f][:], reads=[hq_b[f]])
                                    fcount += 1
                    P.flush()

            with ExitStack() as es:
              if 'B1' in phases:
                def sb(name, shape, dt):
                    return es.enter_context(nc.sbuf_tensor(f"{name}_{l}", shape, dt))
                NQ = 3
                KT = [sb(f"KT{i}", [70, S], BF16) for i in range(2)]
                VA = [sb(f"VA{i}", [128, NB, 128], BF16) for i in range(2)]
                QT = [sb(f"QT{i}", [70, 512], BF16) for i in range(NQ)]
                szf = [sb(f"szf{i}", [64, 512], BF16) for i in range(NQ)]
                pt = [sb(f"pt{i}", [128, 1024], BF16) for i in range(3)]
                rec = [sb(f"rec{i}", [128, 512], F32) for i in range(2)]
                gn = [sb(f"gn{i}", [64, 512], F32) for i in range(2)]
                gf = [sb(f"gf{i}", [64, 512], BF16) for i in range(NQ)]
                KT_b = P.bufs_n(2); VA_b = P.bufs_n(2); QT_b = P.bufs_n(NQ); szf_b = P.bufs_n(NQ); pt_b = P.bufs_n(3)
                st_b = P.bufs_n(3); ot_b = P.bufs_n(2); rec_b = P.bufs_n(2); gn_b = P.bufs_n(2); gf_b = P.bufs_n(NQ)
                for i in range(2):
                    P.op('pool', lambda e, i=i: e.memset(KT[i][64:70, :], 1.0), writes=[KT_b[i]])
                    P.op('pool', lambda e, i=i: e.memset(VA[i][:, :, 64:128], 1.0), writes=[VA_b[i]])
                for i in range(NQ):
                    P.op('pool', lambda e, i=i: e.memset(QT[i][64:70, :], 1.0), writes=[QT_b[i]])
                vview = v_s.rearrange("(kb p) c -> p kb c", p=128)
                tiles = [(h, T) for h in range(8) for T in range(NT)]

                def load_head(h):
                    ks = h % 2
                    P.dma('sp', KT[ks][0:64, :], kT_s[h * 64:(h + 1) * 64, :], writes=[KT_b[ks]])
                    P.dma('sp', KT[ks][67:70, :], augk_s[h], writes=[KT_b[ks]])
                    for c in range(0, NB, 16):
                        ce = min(NB, c + 16)
                        P.dma('sp', VA[ks][:, c:ce, 0:64], vview[:, c:ce, h * 64:(h + 1) * 64], writes=[VA_b[ks]])

                def load_tile(i):
                    h, T = tiles[i]
                    qs = i % NQ
                    cs = slice(T * 512, (T + 1) * 512)
                    P.dma('sp', QT[qs][0:64, :], qT_s[h * 64:(h + 1) * 64, cs], writes=[QT_b[qs]])
                    P.dma('sp', QT[qs][64:67, :], augq_s[h][:, cs], writes=[QT_b[qs]])
                    P.dma('sp', szf[qs][:, :], szfT_s[h * 64:(h + 1) * 64, cs], writes=[szf_b[qs]])

                sti = 0
                load_head(0)
                load_tile(0)
                for i, (h, T) in enumerate(tiles):
                    ks = h % 2
                    qs = i % NQ
                    if T == 0 and h + 1 < 8:
                        load_head(h + 1)
                    if i + 1 < len(tiles):
                        load_tile(i + 1)
                    cs = slice(T * 512, (T + 1) * 512)
                    nkb = 4 * T + 4
                    npair = nkb // 2
                    o = 6 + (i % 2); ob = ot_b[i % 2]
                    w2 = i % 2
                    sts = []

                    def emit_st(pp_):
                        nonlocal sti
                        a = sti % 3; sti += 1
                        for hf in range(2):
                            kb = 2 * pp_ + hf
                            diag = kb >= 4 * T
                            P.op('pe', lambda e, a=a, kb=kb, diag=diag, ks=ks, qs=qs, hf=hf: e.matmul(
                                pbig[a][:, hf * 512:(hf + 1) * 512], lhsT=KT[ks][0:70, kb * 128:(kb + 1) * 128], rhs=QT[qs][0:70, :], start=True, stop=not diag),
                                reads=[KT_b[ks], QT_b[qs]], writes=[st_b[a]])
                            if diag:
                                P.op('pe', lambda e, a=a, j=kb - 4 * T, hf=hf: e.matmul(
                                    pbig[a][:, hf * 512:(hf + 1) * 512], lhsT=ident[:, :], rhs=maskA[:, j, :], start=False, stop=True), writes=[st_b[a]])
                        sts.append(a)

                    emit_st(0)
                    if npair > 1:
                        emit_st(1)
                    for pp_ in range(npair):
                        a = sts[pp_]
                        P.op('act', lambda e, a=a: e.activation(out=pt[a][:, :], in_=pbig[a][:, :], func=AF.Exp),
                             reads=[st_b[a]], writes=[pt_b[a]])
                        if pp_ + 2 < npair:
                            emit_st(pp_ + 2)
                        for hf in range(2):
                            kb = 2 * pp_ + hf
                            P.op('pe', lambda e, a=a, kb=kb, o=o, ks=ks, nkb=nkb, hf=hf: e.matmul(
                                pb[o][:, :], lhsT=VA[ks][:, kb, :], rhs=pt[a][:, hf * 512:(hf + 1) * 512], start=(kb == 0), stop=(kb == nkb - 1)),
                                reads=[VA_b[ks], pt_b[a]], writes=[ob])
                    P.op('dve', lambda e, o=o, w2=w2: e.reciprocal(out=rec[w2][64:128, :], in_=pb[o][64:128, :]), reads=[ob], writes=[rec_b[w2]])
                    P.op('dve', lambda e, o=o, w2=w2: e.tensor_tensor(out=gn[w2][:, :], in0=pb[o][0:64, :], in1=rec[w2][64:128, :], op=ALU.mult),
                         reads=[ob, rec_b[w2]], writes=[gn_b[w2]])
                    P.op('pool', lambda e, qs=qs, w2=w2: e.tensor_tensor(out=gf[qs][:, :], in0=gn[w2][:, :], in1=szf[qs][:, :], op=ALU.mult),
                         reads=[gn_b[w2], szf_b[qs]], writes=[gf_b[qs]])
                    P.dma('pool', gfT_s[h * 64:(h + 1) * 64, cs], gf[qs][:, :], reads=[gf_b[qs]])
                P.flush()

            with ExitStack() as es:
              if 'B2' in phases:
                def sb(name, shape, dt):
                    return es.enter_context(nc.sbuf_tensor(f"{name}_{l}", shape, dt))
                gq = [sb(f"gq{i}", [64, 4, 512], BF16) for i in range(2)]
                gk = [sb(f"gk{i}", [64, 4, 512], BF16) for i in range(2)]
                glr = [sb(f"glr{i}", [16, 512], BF16) for i in range(2)]
                gv = [sb(f"gv{i}", [128, 4, 512], BF16) for i in range(2)]
                szg = [sb(f"szg{i}", [128, 4, 512], BF16) for i in range(2)]
                wg32 = sb("wg32", [16, 256], F32)
                wgb = sb("wgb", [16, 256], BF16)
                bgl = sb("bgl", [64, 4], F32)
                nbg = sb("nbg", [64, 4], F32)
                ggc = sb("ggc", [128, 4], F32)
                e1 = sb("e1", [64, 4, 512], F32)
                la = sb("la", [64, 4, 512], F32)
                cum = sb("cum", [64, 4, 512], F32)
                eq = sb("eq", [64, 4, 512], F32)
                ek = sb("ek", [64, 4, 512], F32)
                qt = sb("qt", [64, 4, 512], BF16)
                kt = sb("kt", [64, 4, 512], BF16)
                ktok = sb("ktok", [128, 16, 64], BF16)
                scm = [sb(f"scm{i}", [128, 4, 128], BF16) for i in range(4)]
                sq = [sb(f"sq{i}", [128, 512], BF16) for i in range(2)]
                l1 = sb("l1", [128, 512], F32)
                rr = sb("rr", [128, 512], F32)
                t1 = sb("t1", [128, 512], F32)
                gg = [sb(f"gg{i}", [128, 512], BF16) for i in range(2)]
                Sf = sb("Sf", [64, 4, 128], F32)
                Sbf = sb("Sbf", [64, 4, 128], BF16)
                tmpS = sb("tmpS", [64, 128], F32)
                in_b = P.bufs_n(2)
                e1_b = P.buf(); la_b = P.buf(); cum_b = P.buf(); eq_b = P.buf(); ek_b = P.buf(); qt_b = P.buf(); kt_b = P.buf()
                ktok_b = P.buf(); tpk_b = P.buf(); scm_b = P.bufs_n(4); sq_b = P.bufs_n(2); l1_b = P.buf(); rr_b = P.buf(); t1_b = P.buf()
                gg_b = P.bufs_n(2); Sf_b = P.bufs_n(4); Sbf_b = P.bufs_n(4); tmpS_b = P.buf()
                otb = P.bufs_n(4)
                scb_ = P.buf()
                U_b = P.buf()
                aux_b = U_b
                d_w = P.dma('pool', wg32[:], w_gg[l])
                d_b = P.dma('pool', bgl[:], b_gla_col[l])
                d_g = P.dma('pool', ggc[:], g_gla_col[l])
                P.op('pool', lambda e: e.tensor_copy(out=wgb[:], in_=wg32[:]), deps=[d_w])
                P.op('pool', lambda e: e.tensor_scalar(out=nbg[:], in0=bgl[:], scalar1=-1.0, scalar2=None, op0=ALU.mult), deps=[d_b])
                P.op('pool', lambda e: e.memset(Sf[:], 0.0))
                P.op('pool', lambda e: e.memset(Sbf[:], 0.0))
                P.flush()
                gqv = gqT_s.rearrange("(h d) s -> d h s", d=64)
                gkv = gkT_s.rearrange("(h d) s -> d h s", d=64)
                gvv = gv_s.rearrange("(c p) e -> p c e", p=128)
                szgv = szgT_s.rearrange("(h e) s -> e h s", e=128)
                ggi = 0
                def b2_load(T):
                    s = T % 2
                    cs = slice(T * 512, (T + 1) * 512)
                    P.dma('sp', gq[s][:], gqv[:, :, cs], writes=[in_b[s]])
                    P.dma('sp', gk[s][:], gkv[:, :, cs], writes=[in_b[s]])
                    P.dma('sp', glr[s][:], glrT_s[:, cs], writes=[in_b[s]])
                    P.dma('sp', gv[s][:], gvv[:, T * 4:(T + 1) * 4, :], writes=[in_b[s]])
                    P.dma('sp', szg[s][:], szgv[:, :, cs], writes=[in_b[s]])

                b2_load(0)
                for T in range(NT):
                    s = T % 2
                    cs = slice(T * 512, (T + 1) * 512)
                    if T + 1 < NT:
                        b2_load(T + 1)
                    for h in range(4):
                        P.op('pe', lambda e, h=h, s=s: e.matmul(pb[5][0:64, :], lhsT=wgb[0:16, h * 64:(h + 1) * 64], rhs=glr[s][0:16, :],
                                                                start=True, stop=True), reads=[in_b[s]], writes=[aux_b])
                        P.op('act', lambda e, h=h: e.activation(out=e1[:, h, :], in_=pb[5][0:64, :], func=AF.Exp, bias=nbg[:, h:h + 1], scale=-1.0),
                             reads=[aux_b], writes=[e1_b])
                    P.op('act', lambda e: e.activation(out=la[:], in_=e1[:], func=AF.Ln, bias=onec[0:64, 0:1], scale=1.0),
                         reads=[e1_b], writes=[la_b])
                    for h in range(4):
                        P.op('dve', lambda e, h=h: e.tensor_tensor_scan(out=cum[:, h, :], data0=segm[:, :], data1=la[:, h, :], initial=0.0,
                                                                        op0=ALU.mult, op1=ALU.add), reads=[la_b], writes=[cum_b])
                    P.op('act', lambda e: e.activation(out=eq[:], in_=cum[:], func=AF.Exp, scale=-1.0 / 16.0), reads=[cum_b], writes=[eq_b])
                    P.op('act', lambda e: e.activation(out=ek[:], in_=cum[:], func=AF.Exp, scale=1.0 / 16.0), reads=[cum_b], writes=[ek_b])
                    P.op('dve', lambda e, s=s: e.scalar_tensor_tensor(out=qt[:], in0=gq[s][:], scalar=0.125, in1=eq[:], op0=ALU.mult, op1=ALU.mult),
                         reads=[in_b[s], eq_b], writes=[qt_b])
                    P.op('dve', lambda e, s=s: e.tensor_tensor(out=kt[:], in0=gk[s][:], in1=ek[:], op=ALU.mult),
                         reads=[in_b[s], ek_b], writes=[kt_b])
                    for c in range(4):
                        for h in range(4):
                            i = c * 4 + h
                            P.op('pe', lambda e, c=c, h=h, i=i: e.transpose(ptb[:, i * 64:(i + 1) * 64], kt[0:64, h, c * 128:(c + 1) * 128], ident[0:64, 0:64]),
                                 reads=[kt_b], writes=[tpk_b])
                    P.op('act', lambda e: e.activation(out=ktok[:].rearrange("p a b -> p (a b)"), in_=ptb[:, :], func=AF.Copy),
                         reads=[tpk_b], writes=[ktok_b])
                    for h in range(4):
                        for c in range(4):
                            P.op('pe', lambda e, h=h, c=c: e.matmul(pb[4][:, c * 128:(c + 1) * 128], lhsT=kt[0:64, h, c * 128:(c + 1) * 128],
                                                                    rhs=qt[0:64, h, c * 128:(c + 1) * 128], start=True, stop=True),
                                 reads=[kt_b, qt_b], writes=[scb_])
                        P.op('dve', lambda e, h=h: e.tensor_tensor(out=scm[h][:].rearrange("p a b -> p (a b)"), in0=pb[4][:, :],
                                                                   in1=cm4[:].rearrange("p a b -> p (a b)"), op=ALU.mult),
                             reads=[scb_], writes=[scm_b[h]])
                    for c in range(4):
                        for h in range(4):
                            ccs = slice(c * 128, (c + 1) * 128)
                            P.op('pe', lambda e, h=h, ccs=ccs: e.matmul(pb[h][:, ccs], lhsT=Sbf[0:64, h, :], rhs=qt[0:64, h, ccs], start=True, stop=False),
                                 reads=[Sbf_b[h], qt_b], writes=[otb[h]])
                            P.op('pe', lambda e, h=h, c=c, ccs=ccs, s=s: e.matmul(pb[h][:, ccs], lhsT=gv[s][:, c, h * 128:(h + 1) * 128], rhs=scm[h][:, c, :],
                                                                              start=False, stop=True),
                                 reads=[scm_b[h], in_b[s]], writes=[otb[h]])
                            P.op('pe', lambda e, h=h, c=c, s=s: e.matmul(pb[5][0:64, 0:128], lhsT=ktok[:, c * 4 + h, :], rhs=gv[s][:, c, h * 128:(h + 1) * 128],
                                                                         start=True, stop=True),
                                 reads=[ktok_b, in_b[s]], writes=[U_b])
                            P.op('dve', lambda e, h=h: e.tensor_tensor(out=tmpS[:, :], in0=pb[5][0:64, 0:128], in1=Sf[:, h, :], op=ALU.add),
                                 reads=[U_b, Sf_b[h]], writes=[tmpS_b])
                            P.op('dve', lambda e, h=h, c=c: e.tensor_scalar(out=Sf[:, h, :], in0=tmpS[:, :], scalar1=eq[:, h, c * 128 + 127:c * 128 + 128],
                                                                            scalar2=None, op0=ALU.mult),
                                 reads=[tmpS_b, eq_b], writes=[Sf_b[h]])
                            P.op('pool', lambda e, h=h: e.tensor_copy(out=Sbf[:, h, :], in_=Sf[:, h, :]), reads=[Sf_b[h]], writes=[Sbf_b[h]])
                    for h in range(4):
                        g = ggi % 2; ggi += 1
                        P.op('act', lambda e, h=h, g=g: e.activation(out=sq[g][:, :], in_=pb[h][:, :], func=AF.Square), reads=[otb[h]], writes=[sq_b[g]])
                        P.op('pe', lambda e, g=g: e.matmul(pb[5][:, :], lhsT=ones_bf[:, :], rhs=sq[g][:, :], start=True, stop=True),
                             reads=[sq_b[g]], writes=[aux_b])
                        P.op('act', lambda e: e.activation(out=l1[:, :], in_=pb[5][:, :], func=AF.Ln, bias=epsc[:, 0:1], scale=1.0 / 128.0),
                             reads=[aux_b], writes=[l1_b])
                        P.op('act', lambda e: e.activation(out=rr[:, :], in_=l1[:, :], func=AF.Exp, scale=-0.5), reads=[l1_b], writes=[rr_b])
                        P.op('dve', lambda e, h=h: e.tensor_tensor(out=t1[:, :], in0=pb[h][:, :], in1=rr[:, :], op=ALU.mult),
                             reads=[otb[h], rr_b], writes=[t1_b])
                        P.op('dve', lambda e, h=h, g=g, s=s: e.scalar_tensor_tensor(out=gg[g][:, :], in0=t1[:, :], scalar=ggc[:, h:h + 1], in1=szg[s][:, h, :],
                                                                                 op0=ALU.mult, op1=ALU.mult),
                             reads=[t1_b, in_b[s]], writes=[gg_b[g]])
                        P.dma('pool', ggT_s[h * 128:(h + 1) * 128, cs], gg[g][:, :], reads=[gg_b[g]])
                P.flush()

            with ExitStack() as es:
              if 'C' in phases:
                def sb(name, shape, dt):
                    return es.enter_context(nc.sbuf_tensor(f"{name}_{l}", shape, dt))
                wbr = sb("wbr", [128, 12, D], BF16)
                wout = sb("wout", [128, 8, D], BF16)
                wpl = sb("wpl", [128, 4, 128], BF16)
                wplw = sb("wplw", [128, 4, 128], BF16)
                wpln = sb("wpln", [128, 4, 128], BF16)
                psc = sb("psc", [128, 4], F32)
                with ExitStack() as es2:
                    wst = [es2.enter_context(nc.sbuf_tensor(f"cwst{i}_{l}", [128, 4, D], F32)) for i in range(2)]
                    wp32 = es2.enter_context(nc.sbuf_tensor(f"wp32_{l}", [128, 4, 128], F32))
                    wst_b = P.bufs_n(2)
                    srcs = [(w_brs[0][l].rearrange("(k p) n -> p k n", p=128), wbr, 0),
                            (w_brs[1][l].rearrange("(k p) n -> p k n", p=128), wbr, 4),
                            (w_brs[2][l].rearrange("(k p) n -> p k n", p=128), wbr, 8),
                            (w_out[l].rearrange("(k p) n -> p k n", p=128)[:, 0:4, :], wout, 0),
                            (w_out[l].rearrange("(k p) n -> p k n", p=128)[:, 4:8, :], wout, 4)]
                    for i, (src, dstt, k0) in enumerate(srcs):
                        s = i % 2
                        P.dma('sp', wst[s][:], src, writes=[wst_b[s]])
                        P.op('pool', lambda e, s=s, dstt=dstt, k0=k0: e.tensor_copy(out=dstt[:, k0:k0 + 4, :], in_=wst[s][:]),
                             reads=[wst_b[s]])
                    d_wp = P.dma('sp', wp32[:], w_pool[l].rearrange("g c d -> c g d"))
                    P.op('pool', lambda e: e.tensor_copy(out=wpl[:], in_=wp32[:]), deps=[d_wp])
                    for g in range(4):
                        P.op('pool', lambda e, g=g: e.tensor_scalar(out=wplw[:, g, :], in0=wp32[:, g, :], scalar1=1.0 / (2, 4, 8, 16)[g], scalar2=None, op0=ALU.mult),
                             deps=[d_wp])
                    P.op('pool', lambda e: e.tensor_scalar(out=wpln[:], in0=wp32[:], scalar1=-1.0, scalar2=None, op0=ALU.mult), deps=[d_wp])
                    P.dma('sp', psc[:], pscale_col[l])
                    P.flush()
                pu = [sb(f"pu{i}", [128, 4, 528], BF16) for i in range(2)]
                szp = [sb(f"szp{i}", [128, 4, 512], BF16) for i in range(2)]
                gfi = [sb(f"gfi{i}", [128, 4, 512], BF16) for i in range(2)]
                ggi_ = [sb(f"ggi{i}", [128, 4, 512], BF16) for i in range(2)]
                sgm = [sb(f"sgm{i}", [128, 3, 512], BF16) for i in range(3)]
                a2 = sb("a2", [128, 3, 32], F32)
                a4 = sb("a4", [128, 2, 32], F32)
                a8 = sb("a8", [128, 32], F32)
                ssum = sb("ssum", [128, 4, 16], F32)
                dl16 = sb("dl16", [128, 4, 16], BF16)
                gp = sb("gp", [128, 4, 512], BF16)
                tt = [[sb(f"tt{i}_{j}", [128, 512], F32) for j in range(3)] for i in range(2)]
                mg = sb("mg", [128, 8, 512], BF16)
                xin = [sb(f"cxin{i}", [128, D], F32) for i in range(2)]
                rr = sb("crr", [128, D], F32)
                yo = [sb(f"yo{i}", [128, D], F32) for i in range(2)]
                sqj = sb("csqj", [128, 512], BF16)
                ss2all = sb("ss2all", [128, NT * 8], F32)
                sst = sb("sst", [128, 1], F32)
                lnv = sb("clnv", [128, 1], F32)
                rstd = sb("crstd", [128, 1], F32)
                inb = P.bufs_n(2); sgm_b = P.bufs_n(3)
                a2_b = P.buf(); a4_b = P.buf(); a8_b = P.buf(); ssum_b = P.buf(); tf_b = P.buf(); dif_b = P.buf(); gp_b = P.buf()
                tt_b = [P.bufs_n(3) for _ in range(2)]
                mg_b = P.buf(); xin_b = P.bufs_n(2); rr_b = P.buf(); yo_b = P.bufs_n(2); sqj_b = P.buf(); ss2_b = P.bufs_n(NT * 4)
                o_z = P.op('pool', lambda e: e.memset(ss2all[:], 0.0), writes=ss2_b)
                sst_b = P.buf(); lnv_b = P.buf(); rstd_b = P.buf()
                ppb = P.buf()
                yb = P.bufs_n(3)
                ob = P.bufs_n(4)
                puv = puT_s.rearrange("(g c) s -> c g s", c=128)
                szpv = szpT_s.rearrange("(g c) s -> c g s", c=128)
                gfv = gfT_s.rearrange("(k p) s -> p k s", p=128)
                ggv = ggT_s.rearrange("(k p) s -> p k s", p=128)
                sgv = sgmT_s.rearrange("(b j p) s -> p b j s", p=128, j=8)
                WIN = (2, 4, 8, 16)
                yi = 0; sgi = 0; xi = 0; tti = 0
                def c_load(T):
                    s = T % 2
                    cs = slice(T * 512, (T + 1) * 512)
                    if T == 0:
                        P.op('pool', lambda e, s=s: e.memset(pu[s][:, :, 0:16], 0.0), writes=[inb[s]])
                        P.dma('sp', pu[s][:, :, 16:528], puv[:, :, cs], writes=[inb[s]])
                    else:
                        P.dma('sp', pu[s][:, :, :], puv[:, :, T * 512 - 16:(T + 1) * 512], writes=[inb[s]])
                    P.dma('sp', szp[s][:], szpv[:, :, cs], writes=[inb[s]])
                    P.dma('sp', gfi[s][:], gfv[:, :, cs], writes=[inb[s]])
                    P.dma('sp', ggi_[s][:], ggv[:, :, cs], writes=[inb[s]])

                def c_loadx(ix):
                    r0_ = ix * 128
                    P.dma('sp', xin[ix % 2][:], xsrc[r0_:r0_ + 128, :], writes=[xin_b[ix % 2]])

                def c_loadsg(ig):
                    T_, j_ = divmod(ig, 8)
                    P.dma('sp', sgm[ig % 3][:], sgv[:, :, j_, T_ * 512:(T_ + 1) * 512], writes=[sgm_b[ig % 3]])

                def c_poolmm(T):
                    s = T % 2
                    u = pu[s]
                    if T == 0:
                        P.op('pool', lambda e, u=u: e.tensor_tensor(out=a2[:, :, 1:32], in0=u[:, 1:4, 1:32], in1=u[:, 1:4, 0:31], op=ALU.add),
                             reads=[inb[s]], writes=[a2_b])
                        P.op('pool', lambda e, u=u: e.tensor_tensor(out=ssum[:, 0, :], in0=u[:, 0, 16:32], in1=u[:, 0, 15:31], op=ALU.add),
                             reads=[inb[s]], writes=[ssum_b])
                        P.op('pool', lambda e: e.tensor_tensor(out=a4[:, :, 3:32], in0=a2[:, 1:3, 3:32], in1=a2[:, 1:3, 1:30], op=ALU.add),
                             reads=[a2_b], writes=[a4_b])
                        P.op('pool', lambda e: e.tensor_tensor(out=ssum[:, 1, :], in0=a2[:, 0, 16:32], in1=a2[:, 0, 14:30], op=ALU.add),
                             reads=[a2_b], writes=[ssum_b])
                        P.op('pool', lambda e: e.tensor_tensor(out=a8[:, 7:32], in0=a4[:, 1, 7:32], in1=a4[:, 1, 3:28], op=ALU.add),
                             reads=[a4_b], writes=[a8_b])
                        P.op('pool', lambda e: e.tensor_tensor(out=ssum[:, 2, :], in0=a4[:, 0, 16:32], in1=a4[:, 0, 12:28], op=ALU.add),
                             reads=[a4_b], writes=[ssum_b])
                        P.op('pool', lambda e: e.tensor_tensor(out=ssum[:, 3, :], in0=a8[:, 16:32], in1=a8[:, 8:24], op=ALU.add),
                             reads=[a8_b], writes=[ssum_b])
                        P.op('pool', lambda e: e.tensor_tensor(out=dl16[:], in0=ssum[:], in1=rcfix[:], op=ALU.mult), reads=[ssum_b], writes=[dif_b])
                    for g in range(4):
                        wn = WIN[g]
                        for i in range(wn):
                            P.op('pe', lambda e, g=g, i=i, u=u: e.matmul(pb[0][:, :], lhsT=wplw[:, g, :], rhs=u[:, g, 16 - i:528 - i], start=(i == 0), stop=False),
                                 reads=[inb[s]], writes=[ppb])
                        P.op('pe', lambda e, g=g, u=u, T=T: e.matmul(pb[0][:, :], lhsT=wpln[:, g, :], rhs=u[:, g, 16:528], start=False, stop=(T != 0)),
                             reads=[inb[s]], writes=[ppb])
                        if T == 0:
                            P.op('pe', lambda e, g=g: e.matmul(pb[0][:, 0:16], lhsT=wpl[:, g, :], rhs=dl16[:, g, :], start=False, stop=True),
                                 reads=[dif_b], writes=[ppb])
                        P.op('dve', lambda e, g=g, s=s: e.scalar_tensor_tensor(out=gp[:, g, :], in0=pb[0][:, :], scalar=psc[:, g:g + 1], in1=szp[s][:, g, :],
                                                                             op0=ALU.mult, op1=ALU.mult),
                             reads=[ppb, inb[s]], writes=[gp_b])

                c_load(0)
                c_loadx(0)
                c_loadsg(0)
                c_loadsg(1)
                c_poolmm(0)
                for T in range(NT):
                    s = T % 2
                    cs = slice(T * 512, (T + 1) * 512)
                    if T + 1 < NT:
                        c_load(T + 1)
                    for j in range(8):
                        q = sgi % 3
                        if sgi + 2 < NT * 8:
                            c_loadsg(sgi + 2)
                        sgi += 1
                        srcs3 = [(gfi[s], 0, inb[s]), (ggi_[s], 4, inb[s]), (gp, 8, gp_b)]
                        w_ = tti % 2; tti += 1
                        for br in range(3):
                            a = yi % 3; yi += 1
                            src, k0, rb = srcs3[br]
                            for k in range(4):
                                P.op('pe', lambda e, a=a, k=k, k0=k0, src=src, j=j: e.matmul(
                                    pb[1 + a][:, :], lhsT=wbr[:, k0 + k, j * 128:(j + 1) * 128], rhs=src[:, k, :], start=(k == 0), stop=(k == 3)),
                                    reads=[rb], writes=[yb[a]])
                            P.op('dve', lambda e, a=a, br=br, q=q, w_=w_: e.tensor_tensor(out=tt[w_][br][:, :], in0=pb[1 + a][:, :], in1=sgm[q][:, br, :], op=ALU.mult),
                                 reads=[yb[a], sgm_b[q]], writes=[tt_b[w_][br]])
                        P.op('dve', lambda e, w_=w_: e.tensor_tensor(out=tt[w_][0][:, :], in0=tt[w_][0][:, :], in1=tt[w_][1][:, :], op=ALU.add),
                             reads=[tt_b[w_][1]], writes=[tt_b[w_][0]])
                        P.op('pool', lambda e, w_=w_, j=j: e.tensor_tensor(out=mg[:, j, :], in0=tt[w_][0][:, :], in1=tt[w_][2][:, :], op=ALU.add),
                             reads=[tt_b[w_][0], tt_b[w_][2]], writes=[mg_b])
                    if T + 1 < NT:
                        c_poolmm(T + 1)
                    for tb in range(4):
                        xs = xi % 2
                        if xi + 1 < NT * 4:
                            c_loadx(xi + 1)
                        xi += 1
                        r0 = T * 512 + tb * 128
                        zc = (T * 4 + tb) * 2
                        zb = ss2_b[T * 4 + tb]
                        op_ = 2 * (tb % 2)
                        for n in range(2):
                            for k in range(8):
                                P.op('pe', lambda e, n=n, k=k, tb=tb, op_=op_: e.matmul(pb[4 + op_ + n][:, :], lhsT=mg[:, k, tb * 128:(tb + 1) * 128],
                                                                               rhs=wout[:, k, n * 512:(n + 1) * 512], start=(k == 0), stop=(k == 7)),
                                     reads=[mg_b], writes=[ob[op_ + n]])
                            P.op('act', lambda e, n=n, zc=zc, op_=op_: e.activation(out=sqj[:, :], in_=pb[4 + op_ + n][:, :], func=AF.Square, accum_out=ss2all[:, zc + n:zc + n + 1]),
                                 reads=[ob[op_ + n]], writes=[sqj_b, zb])
                        P.op('dve', lambda e, zc=zc: e.tensor_tensor(out=sst[:], in0=ss2all[:, zc:zc + 1], in1=ss2all[:, zc + 1:zc + 2], op=ALU.add),
                             reads=[zb], writes=[sst_b])
                        P.op('act', lambda e: e.activation(out=lnv[:], in_=sst[:], func=AF.Ln, bias=epsc[:, 0:1], scale=1.0 / D), reads=[sst_b], writes=[lnv_b])
                        P.op('act', lambda e: e.activation(out=rstd[:], in_=lnv[:], func=AF.Exp, scale=-0.5), reads=[lnv_b], writes=[rstd_b])
                        for n in range(2):
                            P.op('dve', lambda e, n=n, op_=op_: e.scalar_tensor_tensor(out=rr[:, n * 512:(n + 1) * 512], in0=pb[4 + op_ + n][:, :], scalar=rstd[:, 0:1],
                                                                              in1=G2[:, n * 512:(n + 1) * 512], op0=ALU.mult, op1=ALU.mult),
                                 reads=[ob[op_ + n], rstd_b], writes=[rr_b])
                        P.op('pool', lambda e, xs=xs: e.tensor_tensor(out=yo[xs][:], in0=rr[:], in1=xin[xs][:], op=ALU.add),
                             reads=[rr_b, xin_b[xs]], writes=[yo_b[xs]])
                        P.dma('pool', xdst[r0:r0 + 128, :], yo[xs][:], reads=[yo_b[xs]])
                P.flush()
        stats = (P.ninst, P.nwaits)
    return nc, stats


def _consts():
    bf = ml_dtypes.bfloat16
    ident = np.eye(128, dtype=np.float32).astype(bf)
    k = np.arange(128)[:, None, None]
    j = np.arange(4)[None, :, None]
    q = np.arange(512)[None, None, :]
    maskA = np.where(128 * j + k <= q, 0.0, NEG).astype(np.float32).astype(bf)
    s_ = np.arange(128)[:, None, None]
    t_ = np.arange(128)[None, None, :]
    cm4 = np.broadcast_to((s_ <= t_).astype(np.float32), (128, 4, 128)).astype(bf)
    segm = np.ones((64, 512), np.float32)
    segm[:, ::128] = 0.0
    rcfix = np.zeros((128, 4, 16), np.float32)
    for g, w in enumerate((2, 4, 8, 16)):
        rcfix[:, g, :] = 1.0 / np.minimum(np.arange(1, 17), w) - 1.0 / w
    return dict(c_ident=ident, c_maskA=np.ascontiguousarray(maskA), c_cm4=np.ascontiguousarray(cm4), c_segm=segm, c_rcfix=rcfix)


def make_in_maps(inputs, S, DEPTH, ncores):
    f = lambda a: np.ascontiguousarray(np.asarray(a, dtype=np.float32))
    x = f(inputs['x']); c = f(inputs['c'])
    shared = dict(
        w_ada=f(inputs['w_ada'])[:DEPTH],
        b_ada_col=np.ascontiguousarray(f(inputs['b_ada'])[:DEPTH].reshape(DEPTH, 24, 128).transpose(0, 2, 1)),
        b_ada_row=np.ascontiguousarray(f(inputs['b_ada'])[:DEPTH].reshape(DEPTH, 1, 3 * D)),
        g_pre_col=np.ascontiguousarray(f(inputs['g_pre'])[:DEPTH].reshape(DEPTH, 8, 128).transpose(0, 2, 1)),
        g_post_row=np.ascontiguousarray(f(inputs['g_post'])[:DEPTH].reshape(DEPTH, 1, D)),
        w_in=f(inputs['w_in'])[:DEPTH],
        b_forget_col=np.ascontiguousarray(f(inputs['b_forget'])[:DEPTH].reshape(DEPTH, 8, 1)),
        w_gla_gate=f(inputs['w_gla_gate'])[:DEPTH],
        b_gla_col=np.ascontiguousarray(f(inputs['b_gla_gate'])[:DEPTH].reshape(DEPTH, 4, 64).transpose(0, 2, 1)),
        g_gla_col=np.ascontiguousarray(f(inputs['g_gla_norm'])[:DEPTH].reshape(DEPTH, 4, 128).transpose(0, 2, 1)),
        w_pool=f(inputs['w_pool'])[:DEPTH],
        pscale_col=np.ascontiguousarray(f(inputs['pool_scale'])[:DEPTH].reshape(DEPTH, 4, 128).transpose(0, 2, 1)),
        w_br_fox=f(inputs['w_br_fox'])[:DEPTH], w_br_gla=f(inputs['w_br_gla'])[:DEPTH], w_br_pool=f(inputs['w_br_pool'])[:DEPTH],
        w_out=f(inputs['w_out'])[:DEPTH],
    )
    shared.update(_consts())
    maps = []
    for b in range(ncores):
        m = dict(shared)
        m['x'] = np.ascontiguousarray(x[b])
        m['ct'] = np.ascontiguousarray(c[b].reshape(8, 128).T)
        maps.append(m)
    return maps


_CACHE = {}


def kernel(**inputs):
    x = np.asarray(inputs['x'])
    B, S, _ = x.shape
    DEPTH = np.asarray(inputs['w_in']).shape[0]
    key = (S, DEPTH)
    if key not in _CACHE:
        _CACHE[key] = build(S, DEPTH)[0]
    nc = _CACHE[key]
    maps = make_in_maps(inputs, S, DEPTH, B)
    res = run_bass_kernel_spmd(nc, maps, core_ids=list(range(B)))
    return np.stack([np.asarray(r["y"], dtype=np.float32) for r in res.results], axis=0)
```
